# Optimizing a Trainium2 kernel written in Bass

```python
import numpy as np
import jax
import jax.numpy as jnp
from jax import lax

D_MODEL = 1024
BATCH = 8
SEQ = 4096
DEPTH = 1

N_MEM = 256
EPS = 1e-6
MLA_HEADS = 8
Q_LORA = 256
KV_LORA = 128
NOPE_DIM = 64
ROPE_DIM = 32
V_DIM = 64
ROPE_THETA = 10000.0
Q_BLOCK = 128
MLSTM_HEADS = 4
MLSTM_HEAD_DIM = 128
MLSTM_WIDTH = MLSTM_HEADS * MLSTM_HEAD_DIM
MLSTM_CHUNK = 128
MLSTM_CONV = 3
MIX_WIDTH = MLA_HEADS * V_DIM + MLSTM_WIDTH
IN_SPLITS = (Q_LORA, KV_LORA, ROPE_DIM, 2 * MLSTM_WIDTH, MLSTM_WIDTH, MLSTM_WIDTH, 4 * MLSTM_HEADS)
IN_WIDTH = sum(IN_SPLITS)
X_HEADS = 4
X_HEAD_DIM = D_MODEL // X_HEADS
D_FF = ((8 * D_MODEL + 767) // 768) * 256

kernel_name = 'hymba_mla_mlstm_memory_encoder_layer'


def rms_norm(x, g):
    xf = x.astype(jnp.float32)
    y = xf * lax.rsqrt(jnp.mean(xf * xf, axis=-1, keepdims=True) + EPS)
    return (y * g.astype(jnp.float32)).astype(x.dtype)


def rope_tables(positions):
    inv = ROPE_THETA ** (-jnp.arange(0, ROPE_DIM, 2, dtype=jnp.float32) / ROPE_DIM)
    ang = positions.astype(jnp.float32)[..., None] * inv
    return jnp.cos(ang), jnp.sin(ang)


def apply_rope(x, cos, sin):
    xf = x.astype(jnp.float32)
    x1, x2 = jnp.split(xf, 2, axis=-1)
    return jnp.concatenate([x1 * cos - x2 * sin, x2 * cos + x1 * sin], axis=-1).astype(x.dtype)


def mla_attention(z_cq, z_ckv, z_kr, q_norm, w_uq, kv_norm, w_ukv, cos, sin):
    B, S, _ = z_cq.shape
    q = (rms_norm(z_cq, q_norm) @ w_uq).reshape(B, S, MLA_HEADS, NOPE_DIM + ROPE_DIM)
    q_nope = q[..., :NOPE_DIM]
    q_rope = apply_rope(q[..., NOPE_DIM:], cos[:, :, None, :], sin[:, :, None, :])
    kv = (rms_norm(z_ckv, kv_norm) @ w_ukv).reshape(B, S, MLA_HEADS, NOPE_DIM + V_DIM)
    k_nope, v = kv[..., :NOPE_DIM], kv[..., NOPE_DIM:]
    k_rope = apply_rope(z_kr, cos, sin)
    scale = (NOPE_DIM + ROPE_DIM) ** -0.5
    nb = S // Q_BLOCK
    qn = q_nope.reshape(B, nb, Q_BLOCK, MLA_HEADS, NOPE_DIM).swapaxes(0, 1)
    qr = q_rope.reshape(B, nb, Q_BLOCK, MLA_HEADS, ROPE_DIM).swapaxes(0, 1)

    def block(args):
        qn_b, qr_b = args
        s = (jnp.einsum('bqhd,bkhd->bhqk', qn_b, k_nope)
             + jnp.einsum('bqhr,bkr->bhqk', qr_b, k_rope))
        p = jax.nn.softmax(s.astype(jnp.float32) * scale, axis=-1).astype(v.dtype)
        return jnp.einsum('bhqk,bkhd->bqhd', p, v)

    o = lax.map(block, (qn, qr))
    return o.swapaxes(0, 1).reshape(B, S, MLA_HEADS * V_DIM)


def centred_dwconv(u, w):
    K = w.shape[0]
    pad = (K - 1) // 2
    S = u.shape[1]
    up = jnp.pad(u, ((0, 0), (pad, pad), (0, 0)))
    out = up[:, 0:S] * w[0]
    for j in range(1, K):
        out = out + up[:, j:j + S] * w[j]
    return out


def mlstm_direction(q, k, v, i_pre, f_pre):
    B, H, S, Dh = q.shape
    L = MLSTM_CHUNK
    NC = S // L
    qc = q.reshape(B, H, NC, L, Dh)
    kc = k.reshape(B, H, NC, L, Dh)
    vc = v.reshape(B, H, NC, L, Dh)
    logf = jax.nn.log_sigmoid(f_pre).reshape(B, H, NC, L)
    logi = i_pre.reshape(B, H, NC, L)
    b = jnp.cumsum(logf, axis=-1)
    tri = jnp.tril(jnp.ones((L, L), dtype=bool))
    log_d = jnp.where(tri, b[..., :, None] - b[..., None, :] + logi[..., None, :], -jnp.inf)
    w_end = b[..., -1:] - b + logi
    m_loc = jnp.max(w_end, axis=-1)
    e_end = jnp.exp(w_end - m_loc[..., None])
    C_loc = jnp.einsum('bhcs,bhcsd,bhcse->bhcde', e_end, vc, kc)
    n_loc = jnp.einsum('bhcs,bhcse->bhce', e_end, kc)
    b_last = b[..., -1]

    def step(carry, inp):
        C, n, m = carry
        Cl, nl, ml, bl = inp
        m_new = jnp.maximum(bl + m, ml)
        a = jnp.exp(bl + m - m_new)
        c = jnp.exp(ml - m_new)
        C_new = a[..., None, None] * C + c[..., None, None] * Cl
        n_new = a[..., None] * n + c[..., None] * nl
        return (C_new, n_new, m_new), (C, n, m)

    init = (jnp.zeros((B, H, Dh, Dh), jnp.float32),
            jnp.zeros((B, H, Dh), jnp.float32),
            jnp.zeros((B, H), jnp.float32))
    xs = (jnp.moveaxis(C_loc, 2, 0), jnp.moveaxis(n_loc, 2, 0),
          jnp.moveaxis(m_loc, 2, 0), jnp.moveaxis(b_last, 2, 0))
    _, (C_prev, n_prev, m_prev) = lax.scan(step, init, xs)
    C_prev = jnp.moveaxis(C_prev, 0, 2)
    n_prev = jnp.moveaxis(n_prev, 0, 2)
    m_prev = jnp.moveaxis(m_prev, 0, 2)

    log_inter = b + m_prev[..., None]
    m_t = jnp.maximum(log_inter, jnp.max(log_d, axis=-1))
    p = jnp.exp(log_d - m_t[..., None])
    a = jnp.exp(log_inter - m_t)
    s = jnp.einsum('bhctd,bhcsd->bhcts', qc, kc) * p
    num = (jnp.einsum('bhcts,bhcsd->bhctd', s, vc)
           + a[..., None] * jnp.einsum('bhcde,bhcte->bhctd', C_prev, qc))
    den = jnp.sum(s, axis=-1) + a * jnp.einsum('bhce,bhcte->bhct', n_prev, qc)
    h = num / jnp.maximum(jnp.abs(den), jnp.exp(-m_t))[..., None]
    return h.reshape(B, H, S, Dh)


def mlstm_mixer(z_qk, z_v, z_o, z_g, conv_w, gate_bias, out_norm):
    B, S, _ = z_qk.shape
    qk = jax.nn.silu(centred_dwconv(z_qk, conv_w))

    def heads(t):
        return t.astype(jnp.float32).reshape(B, S, MLSTM_HEADS, MLSTM_HEAD_DIM).transpose(0, 2, 1, 3)

    q = heads(qk[..., :MLSTM_WIDTH])
    k = heads(qk[..., MLSTM_WIDTH:]) * (MLSTM_HEAD_DIM ** -0.5)
    v = heads(z_v)
    g = (z_g + gate_bias).astype(jnp.float32).transpose(0, 2, 1)
    i_f, f_f, i_b, f_b = jnp.split(g, 4, axis=1)
    h_fwd = mlstm_direction(q, k, v, i_f, f_f)

    def flip(t):
        return jnp.flip(t, axis=2)

    h_bwd = flip(mlstm_direction(flip(q), flip(k), flip(v), flip(i_b), flip(f_b)))
    h = (h_fwd + h_bwd).transpose(0, 2, 1, 3)
    h = rms_norm(h, out_norm.reshape(MLSTM_HEADS, MLSTM_HEAD_DIM))
    out = jax.nn.sigmoid(z_o.astype(jnp.float32)) * h.reshape(B, S, MLSTM_WIDTH)
    return out.astype(z_o.dtype)


def cross_attention(h, mem_n, w_xq, w_xkv, w_xo):
    B, S, _ = h.shape
    M = mem_n.shape[1]
    q = (h @ w_xq).reshape(B, S, X_HEADS, X_HEAD_DIM)
    kv = (mem_n @ w_xkv).reshape(B, M, 2, X_HEADS, X_HEAD_DIM)
    k, v = kv[:, :, 0], kv[:, :, 1]
    s = jnp.einsum('bqhd,bmhd->bhqm', q, k).astype(jnp.float32) * (X_HEAD_DIM ** -0.5)
    p = jax.nn.softmax(s, axis=-1).astype(v.dtype)
    o = jnp.einsum('bhqm,bmhd->bqhd', p, v).reshape(B, S, X_HEADS * X_HEAD_DIM)
    return o @ w_xo


def swiglu(h, w_gate_up, w_down):
    g, u = jnp.split(h @ w_gate_up, 2, axis=-1)
    return (jax.nn.silu(g) * u) @ w_down


def setup_inputs(seed: int = 0) -> dict:
    key = jax.random.key(seed)
    ks = jax.random.split(key, 26)
    f32 = jnp.float32

    def w(k, shape, fan_in):
        return jax.random.normal(k, shape, f32) * (fan_in ** -0.5)

    def gain(k, shape):
        return 1.0 + 0.02 * jax.random.normal(k, shape, f32)

    x = jax.random.normal(ks[0], (BATCH, SEQ, D_MODEL), f32)
    mem = jax.random.normal(ks[1], (BATCH, N_MEM, D_MODEL), f32)
    offsets = jax.random.randint(ks[2], (BATCH, 1), 0, 2048, dtype=jnp.int32)
    positions = (offsets + jnp.arange(SEQ, dtype=jnp.int32)[None, :]).astype(jnp.int32)
    i_bias_f = 0.1 * jax.random.normal(ks[3], (DEPTH, MLSTM_HEADS), f32)
    i_bias_b = 0.1 * jax.random.normal(ks[4], (DEPTH, MLSTM_HEADS), f32)
    f_base = jnp.linspace(3.0, 6.0, MLSTM_HEADS, dtype=f32)[None, :]
    f_bias_f = f_base + 0.01 * jax.random.normal(ks[5], (DEPTH, MLSTM_HEADS), f32)
    f_bias_b = f_base + 0.01 * jax.random.normal(ks[6], (DEPTH, MLSTM_HEADS), f32)
    mlstm_gate_bias = jnp.concatenate([i_bias_f, f_bias_f, i_bias_b, f_bias_b], axis=-1)
    return {
        'x': x,
        'mem': mem,
        'positions': positions,
        'attn_norm': gain(ks[7], (DEPTH, D_MODEL)),
        'w_in': w(ks[8], (DEPTH, D_MODEL, IN_WIDTH), D_MODEL),
        'q_norm': gain(ks[9], (DEPTH, Q_LORA)),
        'w_uq': w(ks[10], (DEPTH, Q_LORA, MLA_HEADS * (NOPE_DIM + ROPE_DIM)), Q_LORA),
        'kv_norm': gain(ks[11], (DEPTH, KV_LORA)),
        'w_ukv': w(ks[12], (DEPTH, KV_LORA, MLA_HEADS * (NOPE_DIM + V_DIM)), KV_LORA),
        'mlstm_conv': w(ks[13], (DEPTH, MLSTM_CONV, 2 * MLSTM_WIDTH), MLSTM_CONV),
        'mlstm_gate_bias': mlstm_gate_bias,
        'mlstm_norm': gain(ks[14], (DEPTH, MLSTM_WIDTH)),
        'w_out': w(ks[15], (DEPTH, MIX_WIDTH, D_MODEL), MIX_WIDTH),
        'xattn_norm': gain(ks[16], (DEPTH, D_MODEL)),
        'mem_norm': gain(ks[17], (DEPTH, D_MODEL)),
        'w_xq': w(ks[18], (DEPTH, D_MODEL, X_HEADS * X_HEAD_DIM), D_MODEL),
        'w_xkv': w(ks[19], (DEPTH, D_MODEL, 2 * X_HEADS * X_HEAD_DIM), D_MODEL),
        'w_xo': w(ks[20], (DEPTH, X_HEADS * X_HEAD_DIM, D_MODEL), X_HEADS * X_HEAD_DIM),
        'ffn_norm': gain(ks[21], (DEPTH, D_MODEL)),
        'w_gate_up': w(ks[22], (DEPTH, D_MODEL, 2 * D_FF), D_MODEL),
        'w_down': w(ks[23], (DEPTH, D_FF, D_MODEL), D_FF),
        'final_norm': gain(ks[24], (D_MODEL,)),
    }


def reference(x, mem, positions, attn_norm, w_in, q_norm, w_uq, kv_norm, w_ukv,
              mlstm_conv, mlstm_gate_bias, mlstm_norm, w_out, xattn_norm, mem_norm,
              w_xq, w_xkv, w_xo, ffn_norm, w_gate_up, w_down, final_norm):
    cos, sin = rope_tables(positions)
    split_idx = [int(c) for c in np.cumsum(IN_SPLITS)[:-1]]
    for l in range(DEPTH):
        h = rms_norm(x, attn_norm[l])
        z_cq, z_ckv, z_kr, z_qk, z_v, z_o, z_g = jnp.split(h @ w_in[l], split_idx, axis=-1)
        y_mla = mla_attention(z_cq, z_ckv, z_kr, q_norm[l], w_uq[l], kv_norm[l], w_ukv[l], cos, sin)
        y_mlstm = mlstm_mixer(z_qk, z_v, z_o, z_g, mlstm_conv[l], mlstm_gate_bias[l], mlstm_norm[l])
        x = x + jnp.concatenate([y_mla, y_mlstm], axis=-1) @ w_out[l]
        x = x + cross_attention(rms_norm(x, xattn_norm[l]), rms_norm(mem, mem_norm[l]),
                                w_xq[l], w_xkv[l], w_xo[l])
        x = x + swiglu(rms_norm(x, ffn_norm[l]), w_gate_up[l], w_down[l])
    return rms_norm(x, final_norm)
```

```python
import numpy as np
import ml_dtypes
import concourse.bass as bass
import concourse.mybir as mybir
from concourse.bass_utils import run_bass_kernel_spmd

F32 = mybir.dt.float32
BF16 = mybir.dt.bfloat16
I32 = mybir.dt.int32
U8 = mybir.dt.uint8
AF = mybir.ActivationFunctionType
ALU = mybir.AluOpType
AX = mybir.AxisListType

S = 4096
D = 1024
NT = 32
NST = 8
EPS = 1e-6
INW = 2480
DFF = 2816
NMEM = 256
TWO_PI = 2.0 * np.pi
C1 = 6.28125
C2 = TWO_PI - 6.28125


class Prog:
    COMPUTE = ("pe", "act", "dve", "pool")

    def __init__(self):
        self.ops = []
        self.lastw = {}
        self.rd_eng = {}
        self.rd_dma = {}
        self.last_on = {}
        self.dma_keys = []

    def add(self, eng, fn, r=(), w=(), dma=None):
        idx = len(self.ops)
        deps = set()
        for k in r:
            if k in self.lastw:
                deps.add(self.lastw[k])
        for k in w:
            if k in self.lastw:
                deps.add(self.lastw[k])
            deps.update(self.rd_eng.get(k, {}).values())
            deps.update(self.rd_dma.get(k, ()))
        for k in w:
            self.lastw[k] = idx
            self.rd_eng[k] = {}
            self.rd_dma[k] = []
        for k in r:
            if k in w:
                continue
            if dma is not None:
                self.rd_dma.setdefault(k, []).append(idx)
            else:
                self.rd_eng.setdefault(k, {})[eng] = idx
        deps.discard(idx)
        if dma is not None and dma not in self.dma_keys:
            self.dma_keys.append(dma)
        self.ops.append(dict(eng=eng, fn=fn, dma=dma, deps=deps, idx=idx))
        self.last_on[(eng, dma)] = idx
        return idx

    def barrier(self):
        lasts = set(self.last_on.values())
        for eng in ("pe", "act", "dve", "pool", "sp"):
            idx = len(self.ops)
            self.ops.append(dict(eng=eng, fn=None, dma=None, deps=set(lasts), idx=idx))
            self.last_on[(eng, None)] = idx
        self.lastw = {}
        self.rd_eng = {}
        self.rd_dma = {}

    def emit(self, nc, stack):
        ops = self.ops
        need = [False] * len(ops)
        for op in ops:
            for d in op["deps"]:
                p = ops[d]
                if p["dma"] is None and p["eng"] == "pe" and op["eng"] == "pe":
                    continue
                need[d] = True
        sems = {}
        for e in self.COMPUTE:
            sems[e] = stack.enter_context(nc.semaphore("s_" + e))
        dsem = {}
        for k in self.dma_keys:
            dsem[k] = stack.enter_context(nc.semaphore("d_" + str(k)))
        cnt = {e: 0 for e in self.COMPUTE}
        dcnt = {k: 0 for k in self.dma_keys}
        plan = {e: [] for e in ("pe", "act", "dve", "pool", "sp")}
        waited = {e: {} for e in plan}
        for op in ops:
            e = op["eng"]
            waits = {}
            for d in op["deps"]:
                p = ops[d]
                if p["fn"] is None:
                    continue
                if p["dma"] is not None:
                    key = ("d", p["dma"])
                    val = dcnt[p["dma"]]
                else:
                    if p["eng"] == "pe" and e == "pe":
                        continue
                    key = ("c", p["eng"])
                    val = p["val"]
                if waits.get(key, 0) < val:
                    waits[key] = val
            wl = []
            for key, val in waits.items():
                if waited[e].get(key, 0) >= val:
                    continue
                waited[e][key] = val
                wl.append((dsem[key[1]] if key[0] == "d" else sems[key[1]], val))
            sig = None
            if op["fn"] is not None:
                if op["dma"] is not None:
                    dcnt[op["dma"]] += 16
                    sig = (dsem[op["dma"]], 16)
                elif need[op["idx"]]:
                    cnt[e] += 1
                    op["val"] = cnt[e]
                    sig = (sems[e], 1)
                else:
                    op["val"] = cnt[e]
            plan[e].append((wl, op["fn"], sig))
        fin = []
        for e in self.COMPUTE:
            if cnt[e] > 0:
                fin.append((sems[e], cnt[e]))
        for k in self.dma_keys:
            fin.append((dsem[k], dcnt[k]))
        self.stats = dict(n_ops=len(ops), cnt=dict(cnt), n_dma_keys=len(self.dma_keys))

        def run(e_name, eng):
            for wl, fn, sig in plan[e_name]:
                for s, v in wl:
                    eng.wait_ge(s, v)
                if fn is None:
                    continue
                ins = fn(eng)
                if sig is not None:
                    ins.then_inc(sig[0], sig[1])
            if e_name == "sp":
                for s, v in fin:
                    eng.wait_ge(s, v)

        with nc.Block() as block:
            @block.tensor
            def _(eng):
                run("pe", eng)

            @block.scalar
            def _(eng):
                run("act", eng)

            @block.vector
            def _(eng):
                run("dve", eng)

            @block.gpsimd
            def _(eng):
                run("pool", eng)

            @block.sync
            def _(eng):
                run("sp", eng)


class Arena:
    def __init__(self, nc, nbytes):
        self.t = nc.alloc_sbuf_tensor("arena", [128, nbytes], U8)
        self.nbytes = nbytes
        self.off = 0
        self.mark = 0

    def alloc(self, shape, dtype):
        esz = {F32: 4, BF16: 2, I32: 4}[dtype]
        n = 1
        for s_ in shape[1:]:
            n *= s_
        nb = (n * esz + 31) // 32 * 32
        assert self.off + nb <= self.nbytes, ("SBUF arena overflow", self.off, nb)
        v = self.t[0:shape[0], self.off:self.off + nb].bitcast(dtype)[:, 0:n]
        self.off += nb
        if len(shape) == 3:
            v = v.rearrange("p (a b) -> p a b", a=shape[1])
        elif len(shape) == 4:
            v = v.rearrange("p (a b c) -> p a b c", a=shape[1], b=shape[2])
        return v

    def view_at(self, off, shape, dtype):
        save = self.off
        self.off = off
        v = self.alloc(shape, dtype)
        self.off = save
        return v

    def set_mark(self):
        self.mark = self.off

    def reset(self):
        self.off = self.mark


def build_nc(debug=None, STOP=9):
    nc = bass.Bass("TRN2", target_bir_lowering=False)
    import contextlib
    stack = contextlib.ExitStack()
    P = Prog()

    def din(name, shape, dt=F32):
        return nc.dram_tensor(name, list(shape), dt, kind="ExternalInput").ap()

    def dscr(name, shape, dt):
        kind = "ExternalOutput" if (debug and name in debug) else "Internal"
        return nc.dram_tensor(name, list(shape), dt, kind=kind).ap()

    x_d = din("x", [S, D])
    mem_d = din("mem", [NMEM, D])
    pos_d = din("positions", [1, S], I32)
    attn_norm_d = din("attn_norm", [D])
    w_in_d = din("w_in", [D, INW])
    q_norm_d = din("q_norm", [256])
    w_uq_d = din("w_uq", [256, 768])
    kv_norm_d = din("kv_norm", [128])
    w_ukv_d = din("w_ukv", [128, 1024])
    conv_d = din("mlstm_conv", [3, 1024])
    gbias_d = din("mlstm_gate_bias", [1, 16])
    mnorm_d = din("mlstm_norm", [1, 512])
    w_out_d = din("w_out", [D, D])
    xattn_norm_d = din("xattn_norm", [D])
    mem_norm_d = din("mem_norm", [D])
    w_xq_d = din("w_xq", [D, D])
    w_xkv_d = din("w_xkv", [D, 2 * D])
    w_xo_d = din("w_xo", [D, D])
    ffn_norm_d = din("ffn_norm", [D])
    w_gu_d = din("w_gate_up", [D, 2 * DFF])
    w_down_d = din("w_down", [DFF, D])
    fnorm_d = din("final_norm", [1, D])
    identb_d = din("c_identb", [128, 128], BF16)
    identf_d = din("c_identf", [128, 128])
    uincl_d = din("c_uincl", [128, 128])
    uinclT_d = din("c_uinclT", [128, 128])
    invf_d = din("c_invf", [128, 1])
    out_d = nc.dram_tensor("out", [S, D], F32, kind="ExternalOutput").ap()

    qkT_s = dscr("qkT_s", [16, 96, S], BF16)
    v_s = dscr("v_s", [S, 8 * 128], BF16)
    qkmT_s = dscr("qkmT_s", [8, 128, S], BF16)
    vml_s = dscr("vml_s", [S, 4 * 129], BF16)
    o_s = dscr("o_s", [S, 512], F32)
    yT_s = dscr("yT_s", [1024, S], BF16)
    x2_s = dscr("x2_s", [S, D], F32)
    gates_s = dscr("gates_s", [128, 32 * 16], F32)
    tab_s = dscr("tab_s", [64, S], F32)

    A = Arena(nc, 211968)
    psbig = nc.alloc_psum_tensor("psbig", [128, 4096], F32)
    ps = [psbig[:, i * 512:(i + 1) * 512] for i in range(8)]
    ps_rr = [0]

    def psum():
        i = ps_rr[0] % 8
        ps_rr[0] += 1
        return ps[i], ("ps", i)

    dma_rr = [0]

    def dma(out, in_, r, w, key, q="sp", slow=False):
        def fn(e, out=out, in_=in_):
            if slow:
                return e.dma_start(out=out, in_=in_, allow_slow_non_contiguous=True)
            return e.dma_start(out=out, in_=in_)
        return P.add(q, fn, r=r, w=w, dma=key)

    identb = A.alloc([128, 128], BF16)
    identf = A.alloc([128, 128], F32)
    uincl = A.alloc([128, 128], F32)
    uinclT = A.alloc([128, 128], F32)
    onesf = A.alloc([128, 128], F32)
    maskf = A.alloc([128, 128], BF16)
    maskb = A.alloc([128, 128], BF16)
    invf = A.alloc([128, 1], F32)
    gates_all = A.alloc([128, 32, 16], F32)
    dma(identb, identb_d, [], ["identb"], "c0")
    dma(identf, identf_d, [], ["identf"], "c1")
    dma(uincl, uincl_d, [], ["uincl"], "c2")
    dma(uinclT, uinclT_d, [], ["uinclT"], "c3")
    dma(invf, invf_d, [], ["invf"], "c4")
    P.add("pool", lambda e: e.memset(onesf, 1.0), w=["onesf"])
    P.add("dve", lambda e: e.tensor_copy(maskf, uincl), r=["uincl"], w=["maskf"])
    P.add("dve", lambda e: e.tensor_copy(maskb, uinclT), r=["uinclT"], w=["maskb"])
    A.set_mark()

    def gain_cols(src_d, n, name):
        t = A.alloc([128, n], F32)
        dma(t, src_d.rearrange("(c p) -> p c", p=128), [], [name], "g_" + name, slow=True)
        return t

    def rstd_from_ss(ss, sd, rstd, n, key):
        P.add("act", lambda e: e.activation(sd, ss, AF.Sqrt, bias=EPS, scale=1.0 / n),
              r=[key + "ss"], w=[key + "sd"])
        P.add("dve", lambda e: e.reciprocal(rstd, sd), r=[key + "sd"], w=[key + "rstd"])

    def phase_A():
        Wb = A.alloc([128, 8, 2560], BF16)
        xt = [A.alloc([128, D], F32) for _ in range(3)]
        wq = A.alloc([128, 2, 8, 128], BF16)
        wk = A.alloc([128, 8, 96], BF16)
        wv = A.alloc([128, 512], BF16)
        gA = gain_tile(attn_norm_d, D, "gA")
        g_q = gain_cols(q_norm_d, 2, "qn")
        g_kv = gain_cols(kv_norm_d, 1, "kvn")
        gbias = A.alloc([128, 16], F32)
        dma(gbias, gbias_d.partition_broadcast(128), [], ["gbias"], "c5")
        cos2T = A.alloc([32, S], F32)
        sin2T = A.alloc([32, S], F32)
        junk = A.alloc([128, D], BF16)
        xn = [A.alloc([128, D], BF16) for _ in range(4)]
        hT = A.alloc([128, 8, 512], BF16)
        cqn = A.alloc([128, 384], BF16)
        cqnT = A.alloc([128, 3, 512], BF16)
        qk_st = A.alloc([128, 16, 512], BF16)
        krT = A.alloc([32, 512], BF16)
        v_st = A.alloc([128, 4, 8, 128], BF16)
        zext = A.alloc([128, 8, 514], F32)
        qkm_st = A.alloc([128, 8, 512], BF16)
        wconv = A.alloc([128, 3, 8], F32)
        fin0 = A.alloc([128, 8], F32)
        fin1 = A.alloc([128, 8], F32)
        for tap in range(3):
            dma(wconv[:, tap, :], conv_d[tap].rearrange("(c p) -> p c", p=128), [], ["wconv"], "c_wconv", slow=True)
        P.add("pool", lambda e: e.memset(zext[:, :, 0:2], 0.0), w=[("zext", j) for j in range(8)])
        vml_st = A.alloc([128, 4, 4, 129], BF16)
        o_st = A.alloc([128, 4, 512], F32)
        stats = A.alloc([128, 24], F32)
        tA = A.alloc([32, 512], F32)
        tB = A.alloc([32, 512], F32)

        load_w(Wb, w_in_d, 8, INW, None, "Wb")
        n = 0
        for kc in range(8):
            P.add("pool", lambda e, kc=kc: e.tensor_scalar(
                Wb[:, kc, 2480:2496], Wb[:, kc, 400:416], -1.0, None, ALU.mult),
                r=[("Wb", 0, kc)], w=[("Wbr", kc)])
            P.add("pool", lambda e, kc=kc: e.tensor_copy(Wb[:, kc, 2496:2512], Wb[:, kc, 384:400]),
                  r=[("Wb", 0, kc)], w=[("Wbr2", kc)])
        for kc in range(2):
            st = xt[n % 2]
            dma(st[:, 0:768], w_uq_d[kc * 128:(kc + 1) * 128, :], [], [("xt", n % 2)], ("xt", n % 2))
            sv = st[:, 0:768].rearrange("p (h c) -> p h c", c=96)
            g = g_q[:, kc:kc + 1]
            rk = [("xt", n % 2), "qn"]
            P.add("dve", lambda e, sv=sv, kc=kc, g=g: e.tensor_scalar(
                wq[:, kc, :, 0:32], sv[:, :, 64:96], g, None, ALU.mult), r=rk, w=[("wq", kc, 0)])
            P.add("pool", lambda e, sv=sv, kc=kc, g=g: e.tensor_scalar(
                wq[:, kc, :, 32:96], sv[:, :, 0:64], g, None, ALU.mult), r=rk, w=[("wq", kc, 1)])
            P.add("dve", lambda e, sv=sv, kc=kc, g=g: e.tensor_scalar(
                wq[:, kc, :, 96:112], sv[:, :, 80:96], g, -1.0, ALU.mult, ALU.mult), r=rk, w=[("wq", kc, 2)])
            P.add("pool", lambda e, sv=sv, kc=kc, g=g: e.tensor_scalar(
                wq[:, kc, :, 112:128], sv[:, :, 64:80], g, None, ALU.mult), r=rk, w=[("wq", kc, 3)])
            n += 1
        st = xt[n % 2]
        dma(st[:, 0:1024], w_ukv_d[:, :], [], [("xt", n % 2)], ("xt", n % 2))
        sv = st[:, 0:1024].rearrange("p (h c) -> p h c", c=128)
        rk = [("xt", n % 2), "kvn"]
        P.add("pool", lambda e: e.memset(wk[:, :, 0:32], 0.0), w=["wk0"])
        P.add("dve", lambda e, sv=sv: e.tensor_scalar(
            wk[:, :, 32:96], sv[:, :, 0:64], g_kv[:, 0:1], None, ALU.mult), r=rk, w=["wk1"])
        P.add("pool", lambda e, sv=sv: e.tensor_scalar(
            wv.rearrange("p (h c) -> p h c", c=64), sv[:, :, 64:128], g_kv[:, 0:1], None, ALU.mult),
            r=rk, w=["wv"])
        n += 1
        P.add("pool", lambda e: e.memset(v_st[:, :, :, 64:128], 1.0), w=["v_ones"])
        P.add("pool", lambda e: e.memset(vml_st[:, :, :, 128:129], 1.0), w=["vml_ones"])

        off_tabs = A.off
        posi = A.alloc([32, 1024], I32)
        t0 = A.alloc([32, 1024], F32)
        t1 = A.alloc([32, 1024], F32)
        t2 = A.alloc([32, 1024], F32)
        ki = A.alloc([32, 1024], I32)
        for c in range(4):
            cs = slice(c * 1024, (c + 1) * 1024)
            dma(posi, pos_d[0:1, cs].partition_broadcast(32), [], ["posi"], "posi")
            P.add("dve", lambda e: e.tensor_copy(t0, posi), r=["posi"], w=["t0"])
            P.add("dve", lambda e: e.tensor_scalar(t0, t0, invf[0:32, 0:1], None, ALU.mult),
                  r=["t0", "invf"], w=["t0"])
            P.add("dve", lambda e: e.tensor_scalar(ki, t0, 1.0 / TWO_PI, None, ALU.mult),
                  r=["t0"], w=["ki"])
            P.add("dve", lambda e: e.tensor_copy(t1, ki), r=["ki"], w=["t1"])
            P.add("dve", lambda e: e.scalar_tensor_tensor(t0, t1, -C1, t0, ALU.mult, ALU.add),
                  r=["t0", "t1"], w=["t0"])
            P.add("dve", lambda e: e.scalar_tensor_tensor(t0, t1, -C2, t0, ALU.mult, ALU.add),
                  r=["t0", "t1"], w=["t0"])

            def fold(dst, key):
                P.add("dve", lambda e: e.tensor_scalar(t1, dst, float(np.pi), -TWO_PI, ALU.is_gt, ALU.mult),
                      r=[key], w=["t1"])
                P.add("dve", lambda e: e.tensor_tensor(dst, dst, t1, ALU.add), r=[key, "t1"], w=[key])
                P.add("dve", lambda e: e.tensor_scalar(dst, dst, float(-np.pi), float(np.pi), ALU.max, ALU.min),
                      r=[key], w=[key])
            fold(t0, "t0")
            P.add("act", lambda e, cs=cs: e.activation(sin2T[:, cs], t0, AF.Sin), r=["t0"], w=["sin2T"])
            P.add("dve", lambda e: e.tensor_scalar(t2, t0, float(np.pi / 2), None, ALU.add),
                  r=["t0"], w=["t2"])
            fold(t2, "t2")
            P.add("act", lambda e, cs=cs: e.activation(cos2T[:, cs], t2, AF.Sin), r=["t2"], w=["cos2T"])

        cacc = A.view_at(off_tabs, [128, 8, 512], F32)
        P.add("pool", lambda e: e.memset(cacc, 0.0), r=["posi", "t0", "t1", "t2", "ki"],
              w=["posi", "t0", "t1", "t2", "ki"] + [("cacc", j) for j in range(8)])
        xi = 0
        evac_rr = [0]

        def evac(out, in_, r, w):
            eng = "dve" if evac_rr[0] % 4 == 3 else "act"
            evac_rr[0] += 1
            if eng == "act":
                P.add("act", lambda e: e.copy(out, in_), r=r, w=w)
            else:
                P.add("dve", lambda e: e.tensor_copy(out, in_), r=r, w=w)

        xi_box = [0]

        def x_chains(st_i):
            tparts = []
            for t in range(4):
                tile = st_i * 4 + t
                xs = xt[xi_box[0] % 3]
                xk = ("xt", xi_box[0] % 3)
                xnb = xn[xi_box[0] % 4]
                xnk = ("xn", xi_box[0] % 4)
                xi_box[0] += 1
                dma(xs, x_d[tile * 128:(tile + 1) * 128, :], [], [xk], xk)
                ssc = stats[:, 0:1]
                sc = (0, 10, 16, 19)[t]
                xp = "x%d" % t
                P.add("act", lambda e, xs=xs, sc=sc: e.activation(junk, xs, AF.Square, accum_out=stats[:, sc:sc + 1]),
                      r=[xk], w=["junk", xp + "ss"])
                rstd_from_ss(stats[:, sc:sc + 1], stats[:, sc + 1:sc + 2], stats[:, sc + 2:sc + 3], D, xp)
                P.add("dve", lambda e, xs=xs, xnb=xnb, sc=sc: e.scalar_tensor_tensor(xnb, xs, stats[:, sc + 2:sc + 3], gA, ALU.mult, ALU.mult),
                      r=[xk, xp + "rstd", "gA"], w=[xnk])
                def tpart(xnb=xnb, xnk=xnk, t=t):
                    pt, pk = psum()
                    ptv = pt[:].bitcast(BF16).rearrange("p (a b) -> p a b", a=8)

                    def tr(e, xnb=xnb, ptv=ptv):
                        for kc in range(8):
                            ins = e.transpose(ptv[:, kc, :], xnb[:, kc * 128:(kc + 1) * 128], identb)
                        return ins
                    P.add("pe", tr, r=[xnk, "identb"], w=[pk])
                    evac(hT[:, :, t * 128:(t + 1) * 128], ptv, [pk], [("hT", t)])
                tparts.append(tpart)
            return tparts

        for tp_ in x_chains(0):
            tp_()
        for st_i in range(NST):
            tok0 = st_i * 512
            hk = [("hT", t) for t in range(4)]
            wbk = wkeys("Wb", 8, INW)
            for t in range(4):
                tile = st_i * 4 + t
                ts_ = slice(t * 128, (t + 1) * 128)
                pt, pk = psum()

                def mm1(e, pt=pt, ts_=ts_):
                    for kc in range(8):
                        ins = e.matmul(pt[:, 0:384], hT[:, kc, ts_], Wb[:, kc, 0:384], start=(kc == 0), stop=(kc == 7))
                    return ins
                P.add("pe", mm1, r=[("hT", t)] + wkeys("Wb", 8, INW, 0, 384), w=[pk])
                P.add("act", lambda e, pt=pt: e.activation(junk[:, 0:256], pt[:, 0:256], AF.Square, accum_out=stats[:, 4:5]),
                      r=[pk], w=["junk", "css"])
                P.add("act", lambda e, pt=pt: e.activation(junk[:, 256:384], pt[:, 256:384], AF.Square, accum_out=stats[:, 5:6]),
                      r=[pk, "css"], w=["junk", "css"])
                P.add("act", lambda e: e.activation(stats[:, 6:7], stats[:, 4:5], AF.Sqrt, bias=EPS, scale=1.0 / 256),
                      r=["css"], w=["csd"])
                P.add("act", lambda e: e.activation(stats[:, 7:8], stats[:, 5:6], AF.Sqrt, bias=EPS, scale=1.0 / 128),
                      r=["css", "csd"], w=["csd"])
                P.add("dve", lambda e: e.reciprocal(stats[:, 8:10], stats[:, 6:8]), r=["csd"], w=["crstd"])
                P.add("dve", lambda e, pt=pt: e.tensor_scalar(cqn[:, 0:256], pt[:, 0:256], stats[:, 8:9], None, ALU.mult),
                      r=[pk, "crstd"], w=["cqn0"])
                P.add("dve", lambda e, pt=pt: e.tensor_scalar(cqn[:, 256:384], pt[:, 256:384], stats[:, 9:10], None, ALU.mult),
                      r=[pk, "crstd"], w=["cqn1"])
                for (c0, kind) in ((1440, "v"), (1952, "o")):
                    pt4, pk4 = psum()

                    def mm4(e, pt4=pt4, ts_=ts_, c0=c0):
                        for kc in range(8):
                            ins = e.matmul(pt4[:, :], hT[:, kc, ts_], Wb[:, kc, c0:c0 + 512], start=(kc == 0), stop=(kc == 7))
                        return ins
                    P.add("pe", mm4, r=[("hT", t)] + wkeys("Wb", 8, INW, c0, c0 + 512), w=[pk4])
                    if kind == "v":
                        evac(vml_st[:, t, :, 0:128], pt4[:, :].rearrange("p (h c) -> p h c", c=128),
                             [pk4, "vml_ones"], [("vml_st", t)])
                    else:
                        P.add("act", lambda e, pt4=pt4, t=t: e.activation(o_st[:, t, :], pt4[:, :], AF.Sigmoid),
                              r=[pk4], w=[("o_st", t)])
                pt5, pk5 = psum()

                def mm5(e, pt5=pt5, ts_=ts_):
                    for kc in range(8):
                        ins = e.matmul(pt5[:, 0:16], hT[:, kc, ts_], Wb[:, kc, 2464:2480], start=(kc == 0), stop=(kc == 7))
                    return ins
                P.add("pe", mm5, r=[("hT", t)] + wkeys("Wb", 8, INW, 2464, 2480), w=[pk5])
                P.add("dve", lambda e, pt5=pt5, tile=tile: e.tensor_tensor(gates_all[:, tile, :], pt5[:, 0:16], gbias, ALU.add),
                      r=[pk5, "gbias"], w=[("gates", tile)])
                pt2, pk2 = psum()
                ptv2 = pt2[:].bitcast(BF16).rearrange("p (a b) -> p a b", a=8)

                def tr2(e, ptv2=ptv2):
                    for j in range(3):
                        ins = e.transpose(ptv2[:, j, :], cqn[:, j * 128:(j + 1) * 128], identb)
                    return ins
                P.add("pe", tr2, r=["cqn0", "cqn1", "identb"], w=[pk2])
                evac(cqnT[:, :, ts_], ptv2[:, 0:3, :], [pk2], [("cqnT", t)])
                pt3, pk3 = psum()
                P.add("pe", lambda e, pt3=pt3, ts_=ts_: e.matmul(pt3[:, :], cqnT[:, 2, ts_], wv, start=True, stop=True),
                      r=[("cqnT", t), "wv"], w=[pk3])
                evac(v_st[:, t, :, 0:64], pt3[:, :].rearrange("p (h c) -> p h c", c=64), [pk3, "v_ones"], [("v_st", t)])
            cs = slice(tok0, tok0 + 512)
            cqk = [("cqnT", t) for t in range(4)]
            pa, pka = psum()
            pb, pkb = psum()

            def mmkr(e, pa=pa, c0=384):
                for kc in range(8):
                    ins = e.matmul(pa[0:32, :], Wb[:, kc, 384:416], hT[:, kc, :], start=(kc == 0), stop=(kc == 7))
                return ins

            def mmkrr(e, pb=pb):
                for kc in range(8):
                    ins = e.matmul(pb[0:32, :], Wb[:, kc, 2480:2512], hT[:, kc, :], start=(kc == 0), stop=(kc == 7))
                return ins
            P.add("pe", mmkr, r=hk + wkeys("Wb", 8, INW, 384, 416), w=[pka])
            P.add("pe", mmkrr, r=hk + [("Wbr", kc) for kc in range(8)] + [("Wbr2", kc) for kc in range(8)], w=[pkb])

            def rope(pa, pka, pb, pkb, dst, dkey, cs=cs):
                P.add("dve", lambda e: e.tensor_tensor(tA, pa[0:32, :], cos2T[:, cs], ALU.mult),
                      r=[pka, "cos2T"], w=["tA"])
                P.add("dve", lambda e: e.tensor_tensor(tB, pb[0:32, :], sin2T[:, cs], ALU.mult),
                      r=[pkb, "sin2T"], w=["tB"])
                P.add("pool", lambda e: e.tensor_tensor(dst, tA, tB, ALU.add), r=["tA", "tB"], w=[dkey])
            rope(pa, pka, pb, pkb, krT, "krT")
            for h in range(8):
                pq, pkq = psum()
                pr, pkr = psum()

                def mmq(e, pq=pq, h=h):
                    for kc in range(2):
                        ins = e.matmul(pq[0:96, :], wq[:, kc, h, 0:96], cqnT[:, kc, :], start=(kc == 0), stop=(kc == 1))
                    return ins

                def mmr(e, pr=pr, h=h):
                    for kc in range(2):
                        ins = e.matmul(pr[0:32, :], wq[:, kc, h, 96:128], cqnT[:, kc, :], start=(kc == 0), stop=(kc == 1))
                    return ins
                wqk = [("wq", kc, j) for kc in range(2) for j in range(4)]
                P.add("pe", mmq, r=cqk + wqk, w=[pkq])
                P.add("pe", mmr, r=cqk + wqk, w=[pkr])
                P.add("act", lambda e, pq=pq, h=h: e.copy(qk_st[0:96, h, :], pq[0:96, :]),
                      r=[pkq], w=[("qk_st", h)])
                rope(pq, pkq, pr, pkr, qk_st[0:32, h, :], ("qk_st", h))
                pk_, pkk = psum()
                P.add("pe", lambda e, pk_=pk_, h=h: e.matmul(pk_[0:96, :], wk[:, h, :], cqnT[:, 2, :], start=True, stop=True),
                      r=cqk + ["wk0", "wk1"], w=[pkk])
                P.add("act", lambda e, pk_=pk_, h=h: e.copy(qk_st[0:96, 8 + h, :], pk_[0:96, :]),
                      r=[pkk], w=[("qk_st", 8 + h)])
                P.add("pool", lambda e, h=h: e.tensor_copy(qk_st[0:32, 8 + h, :], krT),
                      r=["krT"], w=[("qk_st", 8 + h)])
            dma(qkT_s[:, :, cs].rearrange("j p t -> p j t"), qk_st[0:96, :, :],
                [("qk_st", j) for j in range(16)], ["qkT_s"], "o_qk", q="pool")
            next_tp = x_chains(st_i + 1) if st_i + 1 < NST else []
            for j in range(8):
                pz, pkz = psum()

                def mmz(e, pz=pz, j=j):
                    for kc in range(8):
                        ins = e.matmul(pz[:, :], Wb[:, kc, 416 + j * 128:416 + (j + 1) * 128], hT[:, kc, :],
                                       start=(kc == 0), stop=(kc == 7))
                    return ins
                P.add("pe", mmz, r=hk + wkeys("Wb", 8, INW, 416 + j * 128, 416 + (j + 1) * 128), w=[pkz])
                evac(zext[:, j, 2:514], pz[:, :], [pkz], [("zext", j)])
                P.add("act", lambda e, j=j: e.activation(cacc[:, j, :], zext[:, j, 1:513], AF.Copy, scale=wconv[:, 1, j:j + 1]),
                      r=[("zext", j), "wconv", "qkm_st"], w=[("cacc", j)])
                P.add("dve", lambda e, j=j: e.scalar_tensor_tensor(cacc[:, j, :], zext[:, j, 0:512], wconv[:, 0, j:j + 1], cacc[:, j, :],
                                                                     ALU.mult, ALU.add), r=[("zext", j), "wconv"], w=[("cacc", j)])
                P.add("dve", lambda e, j=j: e.scalar_tensor_tensor(cacc[:, j, :], zext[:, j, 2:514], wconv[:, 2, j:j + 1], cacc[:, j, :],
                                                                     ALU.mult, ALU.add), r=[("zext", j), "wconv"], w=[("cacc", j)])
            ck = [("cacc", j) for j in range(8)]
            zk_ = [("zext", j) for j in range(8)]
            P.add("act", lambda e: e.activation(qkm_st[:].rearrange("p a b -> p (a b)"), cacc[:].rearrange("p a b -> p (a b)"), AF.Silu),
                  r=ck, w=["qkm_st"])
            if st_i == 0:
                dma(qkmT_s[:, :, 0:511].rearrange("j p t -> p j t"), qkm_st[:, :, 1:512], ["qkm_st"], ["qkmT_s"], "o_zqk", q="pool")
            else:
                dma(qkmT_s[:, :, tok0 - 1:tok0 + 511].rearrange("j p t -> p j t"), qkm_st[:, :, 0:512], ["qkm_st"], ["qkmT_s"],
                    "o_zqk", q="pool")
            P.add("pool", lambda e: e.tensor_copy(zext[:, :, 0:2], zext[:, :, 512:514]), r=zk_, w=zk_)
            for tp_ in next_tp:
                tp_()
            dma(v_s[tok0:tok0 + 512, :].rearrange("(a p) c -> p a c", p=128),
                v_st[:].rearrange("p a h c -> p a (h c)"),
                [("v_st", t) for t in range(4)], ["v_s"], "o_v", q="pool")
            dma(vml_s[tok0:tok0 + 512, :].rearrange("(a p) c -> p a c", p=128),
                vml_st[:].rearrange("p a h c -> p a (h c)"),
                [("vml_st", t) for t in range(4)], ["vml_s"], "o_vml", q="pool")
            dma(o_s[tok0:tok0 + 512, :].rearrange("(a p) c -> p a c", p=128), o_st,
                [("o_st", t) for t in range(4)], ["o_s"], "o_o", q="pool")
        zk_ = [("zext", j) for j in range(8)]
        P.add("dve", lambda e: e.tensor_tensor(fin0, zext[:, :, 0], wconv[:, 0, :], ALU.mult), r=zk_ + ["wconv"], w=["fin0"])
        P.add("dve", lambda e: e.tensor_tensor(fin1, zext[:, :, 1], wconv[:, 1, :], ALU.mult), r=zk_ + ["wconv"], w=["fin1"])
        P.add("dve", lambda e: e.tensor_tensor(fin0, fin0, fin1, ALU.add), r=["fin0", "fin1"], w=["fin0"])
        P.add("act", lambda e: e.activation(qkm_st[:, :, 0], fin0, AF.Silu), r=["fin0", "qkm_st"], w=["qkm_st"])
        dma(qkmT_s[:, :, S - 1:S].rearrange("j p t -> p j t"), qkm_st[:, :, 0:1], ["qkm_st"], ["qkmT_s"], "o_zqk", q="pool", slow=True)
        if debug and "tab_s" in debug:
            dma(tab_s[0:32, :], cos2T, ["cos2T"], ["tab_s0"], "o_t0")
            dma(tab_s[32:64, :], sin2T, ["sin2T"], ["tab_s1"], "o_t1")
        if debug and "gates_s" in debug:
            dma(gates_s, gates_all[:].rearrange("p a b -> p (a b)"),
                [("gates", i) for i in range(32)], ["gates_s"], "o_g")

    def phase_B():
        Vall = A.alloc([128, 32, 1024], BF16)
        dma(Vall, v_s.rearrange("(a p) c -> p a c", p=128), ["v_s"], ["Vall"], "b_v")
        qT = [A.alloc([96, S], BF16) for _ in range(2)]
        kT = [A.alloc([96, S], BF16) for _ in range(2)]
        NPT = 4
        SB = [0, 2, 6]
        pT = [A.alloc([128, 1024], BF16) for _ in range(NPT)]
        o_sb = [A.alloc([96, 512], F32) for _ in range(2)]
        rs = A.alloc([96, 512], F32)
        yT = [A.alloc([64, S], BF16) for _ in range(2)]
        scale = float(96 ** -0.5)

        def load_head(h):
            dma(qT[h % 2], qkT_s[h], ["qkT_s"], [("qT", h % 2)], ("qT", h % 2))
            dma(kT[h % 2], qkT_s[8 + h], ["qkT_s"], [("kT", h % 2)], ("kT", h % 2), q="pool")

        load_head(0)
        pending = []
        blk = 0
        cnt = [0]
        for h in range(8):
            if h + 1 < 8:
                load_head(h + 1)
            q_, k_, y_ = qT[h % 2], kT[h % 2], yT[h % 2]
            qk_, kk_, yk_ = ("qT", h % 2), ("kT", h % 2), ("yT", h % 2)
            for j in range(8):
                po, pko = ps[4 + blk % 2], ("ps", 4 + blk % 2)
                osb, osk = o_sb[blk % 2], ("o_sb", blk % 2)
                blk += 1
                qs = slice(j * 512, (j + 1) * 512)

                def s_pair(ii, q_=q_, k_=k_, qs=qs, qk_=qk_, kk_=kk_):
                    n = cnt[0]
                    cnt[0] += 1
                    b = SB[n % 3]

                    def fn(e, b=b, ii=ii):
                        e.matmul(ps[b], k_[:, (2 * ii) * 128:(2 * ii + 1) * 128], q_[:, qs], start=True, stop=True)
                        return e.matmul(ps[b + 1], k_[:, (2 * ii + 1) * 128:(2 * ii + 2) * 128], q_[:, qs], start=True, stop=True)
                    P.add("pe", fn, r=[qk_, kk_], w=[("psp", b)])
                    return n
                ns = [s_pair(0), s_pair(1)]
                for ii in range(16):
                    if ii + 2 < 16:
                        ns.append(s_pair(ii + 2))
                    n = ns[ii]
                    b = SB[n % 3]
                    pt_ = pT[n % NPT]
                    ptk = ("pT", n % NPT)
                    P.add("act", lambda e, b=b, pt_=pt_: e.activation(pt_, psbig[:, b * 512:(b + 2) * 512], AF.Exp, scale=scale),
                          r=[("psp", b)], w=[ptk])

                    def pv(e, pt_=pt_, ii=ii, po=po, h=h):
                        e.matmul(po[:, :], Vall[:, 2 * ii, h * 128:(h + 1) * 128], pt_[:, 0:512], start=(ii == 0), stop=False)
                        return e.matmul(po[:, :], Vall[:, 2 * ii + 1, h * 128:(h + 1) * 128], pt_[:, 512:1024],
                                        start=False, stop=(ii == 15))
                    P.add("pe", pv, r=[ptk, "Vall"], w=[pko])
                    if ii == 7 and pending:
                        pending.pop(0)()

                def fin(po=po, pko=pko, osb=osb, osk=osk, y_=y_, yk_=yk_, qs=qs, j=j, h=h):
                    P.add("dve", lambda e: e.tensor_copy(osb[0:96, :], po[0:96, :]), r=[pko], w=[osk])
                    P.add("dve", lambda e: e.reciprocal(rs[64:65, :], osb[64:65, :]), r=[osk], w=["rs"])
                    P.add("pe", lambda e: e.matmul(po[0:64, :], onesf[64:65, 0:64], rs[64:65, :], start=True, stop=True),
                          r=["rs", "onesf"], w=[pko])
                    P.add("dve", lambda e: e.tensor_tensor(y_[0:64, qs], osb[0:64, :], po[0:64, :], ALU.mult),
                          r=[osk, pko], w=[(yk_, j), pko])
                    if j == 7:
                        dma(yT_s[h * 64:(h + 1) * 64, :], y_[0:64, :], [(yk_, jj) for jj in range(8)], ["yT_s"], "o_yT")
                pending.append(fin)
        while pending:
            pending.pop(0)()

    def phase_C():
        vml = A.alloc([128, 32, 516], BF16)
        dma(vml, vml_s.rearrange("(a p) c -> p a c", p=128), ["vml_s"], ["vml"], "c_vml")
        gmn = A.alloc([128, 512], F32)
        dma(gmn, mnorm_d.partition_broadcast(128), [], ["gmn"], "c_gmn")
        L = A.alloc([128, 2, 32, 4], F32)
        E = A.alloc([128, 2, 32, 4], F32)
        Tsb = A.alloc([128, 2, 32, 4], F32)
        Dm = A.alloc([128, 2, 32, 4], F32)
        Wp = A.alloc([128, 2, 32, 4], F32)
        Fi = A.alloc([128, 2, 32, 4], F32)
        Ec = A.alloc([128, 2, 32, 4], F32)
        gk = [("gates", i) for i in range(32)]
        for d in range(2):
            fsl = gates_all[:, :, 4 + 8 * d:8 + 8 * d]
            isl = gates_all[:, :, 8 * d:4 + 8 * d]
            P.add("act", lambda e, d=d, fsl=fsl: e.activation(E[:, d], fsl, AF.Exp, scale=-1.0), r=gk, w=[("E", d)])
            P.add("act", lambda e, d=d: e.activation(L[:, d], E[:, d], AF.Ln, bias=1.0), r=[("E", d)], w=[("L", d)])
            Lf = L[:, d].rearrange("p a b -> p (a b)")
            pb_, pkb_ = psum()
            pt_, pkt_ = psum()
            um = uincl if d == 0 else uinclT
            P.add("pe", lambda e, pb_=pb_, um=um, Lf=Lf: e.matmul(pb_[:, 0:128], um, Lf, start=True, stop=True),
                  r=[("L", d), "uincl", "uinclT"], w=[pkb_])
            P.add("pe", lambda e, pt_=pt_, Lf=Lf: e.matmul(pt_[:, 0:128], onesf, Lf, start=True, stop=True),
                  r=[("L", d), "onesf"], w=[pkt_])
            Tf = Tsb[:, d].rearrange("p a b -> p (a b)")
            Df = Dm[:, d].rearrange("p a b -> p (a b)")
            P.add("act", lambda e, pt_=pt_, Tf=Tf: e.copy(Tf, pt_[:, 0:128]), r=[pkt_], w=[("T", d)])
            P.add("dve", lambda e, pb_=pb_, Tf=Tf, Df=Df: e.tensor_tensor(Df, pb_[:, 0:128], Tf, ALU.subtract),
                  r=[pkb_, ("T", d)], w=[("Dm", d)])
            P.add("act", lambda e, d=d: e.activation(Fi[:, d], Dm[:, d], AF.Exp), r=[("Dm", d)], w=[("Ft", d)])
            P.add("act", lambda e, d=d: e.activation(Ec[:, d], Tsb[:, d], AF.Exp, scale=-1.0), r=[("T", d)], w=[("Ec", d)])
            P.add("dve", lambda e, d=d, isl=isl: e.tensor_tensor(E[:, d], Dm[:, d], isl, ALU.add),
                  r=[("Dm", d), ("L", d)] + gk, w=[("E", d)])
            P.add("act", lambda e, d=d: e.activation(Wp[:, d], E[:, d], AF.Exp), r=[("E", d)], w=[("Wp", d)])

        cK = float(128 ** -0.5)
        ucf = A.alloc([128, 128], F32)
        ucb = A.alloc([128, 128], F32)
        P.add("dve", lambda e: e.tensor_scalar(ucf, uincl, cK, None, ALU.mult), r=["uincl"], w=["ucf"])
        P.add("dve", lambda e: e.tensor_scalar(ucb, uinclT, cK, None, ALU.mult), r=["uinclT"], w=["ucb"])
        Ecc = A.alloc([128, 2, 32, 4], F32)
        P.add("dve", lambda e: e.tensor_scalar(Ecc[:].rearrange("p a b c -> p (a b c)"), Ec[:].rearrange("p a b c -> p (a b c)"),
                                               cK, None, ALU.mult), r=[("Ec", 0), ("Ec", 1)], w=["Ecc"])
        qTm = A.alloc([128, S], BF16)
        kTm = A.alloc([128, S], BF16)
        ktok = A.alloc([128, 32, 128], BF16)
        o_h = A.alloc([128, 32, 128], F32)
        vt_all = [A.alloc([128, 32, 129], BF16) for _ in range(2)]
        hraw2 = [[A.alloc([128, 32, 129], F32) for _ in range(2)] for _ in range(2)]
        yb = A.alloc([128, 32, 128], BF16)
        yTh = A.alloc([128, S], BF16)
        CT32 = [A.alloc([128, 129], F32) for _ in range(2)]
        CTs = [[A.alloc([128, 129], BF16) for _ in range(2)] for _ in range(2)]
        sTm = [A.alloc([128, 128], BF16) for _ in range(4)]
        nd_ = A.alloc([128, 32, 1], F32)
        rr_ = [A.alloc([128, 32, 1], F32) for _ in range(2)]
        ssq = A.alloc([128, 32, 1], F32)
        ssd = A.alloc([128, 32, 1], F32)
        srs = A.alloc([128, 32, 1], F32)
        gate_keys = [("Wp", 0), ("Wp", 1), ("Ft", 0), ("Ft", 1), ("Ec", 0), ("Ec", 1), "Ecc"]
        step = 0
        def prep(h):
            dma(qTm, qkmT_s[h], ["qkmT_s"], ["qTm"], "c_zq")
            dma(kTm, qkmT_s[4 + h], ["qkmT_s"], ["kTm"], "c_zk", q="pool")
            for c4 in range(8):
                pt, pk = psum()
                ptv = pt[:].bitcast(BF16).rearrange("p (a b) -> p a b", a=8)

                def trk(e, ptv=ptv, c4=c4):
                    for a in range(4):
                        c = c4 * 4 + a
                        ins = e.transpose(ptv[:, a, :], kTm[:, c * 128:(c + 1) * 128], identb)
                    return ins
                P.add("pe", trk, r=["kTm", "identb"], w=[pk])
                P.add("act", lambda e, ptv=ptv, c4=c4: e.copy(ktok[:, c4 * 4:(c4 + 1) * 4, :], ptv[:, 0:4, :]),
                      r=[pk], w=[("ktok", c4)])
            kkeys = [("ktok", c4) for c4 in range(8)]
            vview = vml[:, :, h * 129:(h + 1) * 129]
            for d in range(2):
                wb_ = Wp[:, d, :, h:h + 1].to_broadcast([128, 32, 129])
                P.add("pool", lambda e, d=d, wb_=wb_, vview=vview: e.tensor_tensor(vt_all[d], vview, wb_, ALU.mult),
                      r=["vml"] + gate_keys, w=[("vt_all", d)])
                P.add("pool", lambda e, d=d: e.memset(CT32[d], 0.0), w=[("CT32", d)])
                P.add("pool", lambda e, d=d: e.memset(CTs[d][0], 0.0), w=[("CTs", d, 0)])

        def loop(h, pend):
            hraw = hraw2[h % 2]
            hs_ = h % 2
            kkeys = [("ktok", c4) for c4 in range(8)]
            orders = [list(range(32)), list(range(31, -1, -1))]
            for n in range(32):
                for d in range(2):
                    um = ucf if d == 0 else ucb
                    order = orders[d]
                    c = order[n]
                    csl = slice(c * 128, (c + 1) * 128)
                    sl_ = step_box[0] % 4
                    step_box[0] += 1
                    pS, pkS = psum()
                    P.add("pe", lambda e, pS=pS, csl=csl: e.matmul(pS[:, 0:128], kTm[:, csl], qTm[:, csl], start=True, stop=True),
                          r=["kTm", "qTm"], w=[pkS])
                    P.add("dve", lambda e, pS=pS, sl_=sl_, um=um: e.tensor_tensor(sTm[sl_], pS[:, 0:128], um, ALU.mult),
                          r=[pkS, "ucf", "ucb"], w=[("sTm", sl_)])
                    pK, pkK = psum()
                    P.add("pe", lambda e, pK=pK, c=c, d=d: e.matmul(pK[:, 0:129], ktok[:, c, :], vt_all[d][:, c, :], start=True, stop=True),
                          r=kkeys + [("vt_all", d)], w=[pkK])
                    pO, pkO = psum()

                    def mmo(e, pO=pO, sl_=sl_, csl=csl, n=n, d=d, c=c):
                        e.matmul(pO[:, 0:129], sTm[sl_], vt_all[d][:, c, :], start=True, stop=False)
                        return e.matmul(pO[:, 0:129], qTm[:, csl], CTs[d][n % 2], start=False, stop=True)
                    P.add("pe", mmo, r=[("sTm", sl_), ("vt_all", d), "qTm", ("CTs", d, n % 2)], w=[pkO])
                    P.add("dve", lambda e, pK=pK, c=c, h=h, d=d: e.scalar_tensor_tensor(
                        CT32[d], CT32[d], Ec[:, d, c, h:h + 1], pK[:, 0:129], ALU.mult, ALU.add),
                        r=[("CT32", d), pkK] + gate_keys, w=[("CT32", d)])
                    if n + 1 < 32:
                        cn = order[n + 1]
                        P.add("act", lambda e, n=n, cn=cn, h=h, d=d: e.activation(
                            CTs[d][(n + 1) % 2], CT32[d], AF.Copy, scale=Ecc[:, d, cn, h:h + 1]),
                            r=[("CT32", d)] + gate_keys, w=[("CTs", d, (n + 1) % 2)])
                    P.add("act", lambda e, pO=pO, d=d, c=c, hraw=hraw: e.copy(hraw[d][:, c, :], pO[:, 0:129]), r=[pkO],
                          w=[("hraw", hs_, d, c)])
                    if pend:
                        pend.pop(0)()
            while pend:
                pend.pop(0)()

        def load_oh(h):
            dma(o_h, o_s[:, h * 128:(h + 1) * 128].rearrange("(a p) c -> p a c", p=128), ["o_s"], ["o_h"], "c_oh", q="pool")
            gm = gmn[:, h * 128:(h + 1) * 128].unsqueeze(1).to_broadcast([128, 32, 128])
            P.add("pool", lambda e, gm=gm: e.tensor_tensor(o_h, o_h, gm, ALU.mult), r=["o_h", "gmn"], w=["o_h"])

        def post_ops(h):
            hs_ = h % 2
            hr = hraw2[hs_]
            H0 = hr[0][:, :, 0:128]
            H1 = hr[1][:, :, 0:128]
            ops = []
            for d in (1, 0):
                hk_d = [("hraw", hs_, d, c) for c in range(32)]
                den = hr[d][:, :, 128:129]
                fi = Fi[:, d, :, h:h + 1]
                ops.append(lambda den=den, hk_d=hk_d: P.add("dve", lambda e: e.tensor_scalar(nd_, den, -1.0, None, ALU.mult), r=hk_d, w=["nd_"]))
                ops.append(lambda den=den, hk_d=hk_d: P.add("dve", lambda e: e.tensor_tensor(nd_, nd_, den, ALU.max), r=hk_d + ["nd_"], w=["nd_"]))
                ops.append(lambda fi=fi: P.add("dve", lambda e: e.tensor_tensor(nd_, nd_, fi, ALU.max), r=["nd_"] + gate_keys, w=["nd_"]))
                ops.append(lambda d=d: P.add("dve", lambda e: e.reciprocal(rr_[d], nd_), r=["nd_"], w=[("rr", d)]))
            hk0 = [("hraw", hs_, 0, c) for c in range(32)]
            hk1 = [("hraw", hs_, 1, c) for c in range(32)]
            K0, K1 = ("H", hs_, 0), ("H", hs_, 1)
            ops.append(lambda: P.add("pool", lambda e: e.tensor_tensor(H1, H1, rr_[1].to_broadcast([128, 32, 128]), ALU.mult),
                                     r=hk1 + [("rr", 1)], w=hk1 + [K1]))
            ops.append(lambda: P.add("dve", lambda e: e.tensor_tensor(H0, H0, rr_[0].to_broadcast([128, 32, 128]), ALU.mult),
                                     r=hk0 + [("rr", 0)], w=hk0 + [K0]))
            ops.append(lambda: P.add("dve", lambda e: e.tensor_tensor(H0, H0, H1, ALU.add), r=[K0, K1], w=[K0]))
            ops.append(lambda: P.add("act", lambda e: e.activation(H1, H0, AF.Square), r=[K0, K1], w=[K1]))
            ops.append(lambda: P.add("dve", lambda e: e.tensor_reduce(ssq[:, :, 0], H1, AX.X, ALU.add), r=[K1], w=["ssq"]))
            ops.append(lambda: P.add("act", lambda e: e.activation(ssd, ssq, AF.Sqrt, bias=EPS, scale=1.0 / 128), r=["ssq"], w=["ssd"]))
            ops.append(lambda: P.add("dve", lambda e: e.reciprocal(srs, ssd), r=["ssd"], w=["srs"]))
            ops.append(lambda: P.add("dve", lambda e: e.tensor_tensor(H0, H0, srs.to_broadcast([128, 32, 128]), ALU.mult),
                                     r=[K0, "srs"], w=[K0]))
            ops.append(lambda: P.add("dve", lambda e: e.tensor_tensor(yb, H0, o_h, ALU.mult), r=[K0, "o_h"], w=["yb"] + hk0 + hk1))
            if h + 1 < 4:
                ops.append(lambda: load_oh(h + 1))
            for c4 in range(8):
                def trg(c4=c4):
                    pt, pk = psum()
                    ptv = pt[:].bitcast(BF16).rearrange("p (a b) -> p a b", a=8)

                    def try_(e, ptv=ptv, c4=c4):
                        for a in range(4):
                            ins = e.transpose(ptv[:, a, :], yb[:, c4 * 4 + a, :], identb)
                        return ins
                    P.add("pe", try_, r=["yb", "identb"], w=[pk])
                    P.add("act", lambda e, ptv=ptv, c4=c4: e.copy(
                        yTh[:, c4 * 512:(c4 + 1) * 512].rearrange("p (a b) -> p a b", a=4), ptv[:, 0:4, :]),
                        r=[pk], w=[("yTh", c4)])
                ops.append(trg)
            ops.append(lambda: dma(yT_s[512 + h * 128:512 + (h + 1) * 128, :], yTh, [("yTh", c4) for c4 in range(8)], ["yT_s"], "o_yT2"))
            return ops

        step_box = [0]
        prep(0)
        load_oh(0)
        pend = []
        for h in range(4):
            loop(h, pend)
            if h + 1 < 4:
                prep(h + 1)
            pend = post_ops(h)
        while pend:
            pend.pop(0)()

    wl_n = [0]
    PW = 640

    WPW = {}

    def load_w(dst, src_d, KC, N, gain, wkey, stage=None, skey=None):
        assert gain is None
        npc = (N + 2047) // 2048
        pw = (N + npc - 1) // npc
        WPW[wkey] = pw
        for pi in range(npc):
            c0, c1 = pi * pw, min(N, (pi + 1) * pw)
            for kc in range(KC):
                dma(dst[:, kc, c0:c1], src_d[kc * 128:(kc + 1) * 128, c0:c1], [], [(wkey, pi, kc)], (wkey + "_d", pi), q="pool")

    def wkeys(wkey, KC, N, c0=0, c1=None):
        c1 = N if c1 is None else c1
        pw = WPW[wkey]
        pis = range(c0 // pw, (c1 - 1) // pw + 1)
        return [(wkey, pi, kc) for pi in pis for kc in range(KC)]

    def gain_tile(src_d, n, name):
        t = A.alloc([128, n], F32)
        dma(t, src_d.rearrange("(o d) -> o d", o=1).partition_broadcast(128), [], [name], "gt_" + name)
        return t

    def norm_transpose(xs, xk, xnb, xnk, junk, stats, dstT, dkey, pfx, col=0, gain=None, defer=False, junk_key="junk"):
        xks = list(xk) if isinstance(xk, list) else [xk]
        pfx = pfx + str(col)
        c0 = 3 * col
        P.add("act", lambda e: e.activation(junk, xs, AF.Square, accum_out=stats[:, c0:c0 + 1]), r=xks, w=[junk_key, pfx + "ss"])
        rstd_from_ss(stats[:, c0:c0 + 1], stats[:, c0 + 1:c0 + 2], stats[:, c0 + 2:c0 + 3], D, pfx)
        if gain is None:
            P.add("dve", lambda e: e.tensor_scalar(xnb, xs, stats[:, c0 + 2:c0 + 3], None, ALU.mult), r=xks + [pfx + "rstd"], w=[xnk])
        else:
            gt_, gk_ = gain
            P.add("dve", lambda e: e.scalar_tensor_tensor(xnb, xs, stats[:, c0 + 2:c0 + 3], gt_, ALU.mult, ALU.mult),
                  r=xks + [pfx + "rstd", gk_], w=[xnk])
        def tpart():
            pt, pk = psum()
            ptv = pt[:].bitcast(BF16).rearrange("p (a b) -> p a b", a=8)

            def tr(e):
                for kc in range(8):
                    ins = e.transpose(ptv[:, kc, :], xnb[:, kc * 128:(kc + 1) * 128], identb)
                return ins
            P.add("pe", tr, r=[xnk, "identb"], w=[pk])
            P.add("act", lambda e: e.copy(dstT, ptv), r=[pk], w=[dkey])
        if defer:
            return tpart
        tpart()

    def phase_D():
        Wo = A.alloc([128, 8, 1024], BF16)
        Wxq = A.alloc([128, 8, 1024], BF16)
        Wxo = A.alloc([128, 8, 1024], BF16)
        kxT = A.alloc([128, 8, 256], BF16)
        vx = A.alloc([128, 2, 1024], BF16)
        onesb = A.alloc([128, 128], BF16)
        P.add("pool", lambda e: e.memset(onesb, 1.0), w=["onesb"])
        gX = gain_tile(xattn_norm_d, D, "gX")
        gM = gain_tile(mem_norm_d, D, "gM")
        stats = A.alloc([128, 16], F32)
        junk = A.alloc([128, D], BF16)
        xn = [A.alloc([128, D], BF16) for _ in range(4)]
        xt = [A.alloc([128, D], F32) for _ in range(2)]
        Wxkv = A.alloc([128, 8, 2048], BF16)
        memnT = A.alloc([128, 8, 256], BF16)
        xm = [A.alloc([128, D], F32) for _ in range(2)]
        load_w(Wo, w_out_d, 8, 1024, None, "Wo")
        load_w(Wxkv, w_xkv_d, 8, 2048, None, "Wxkv")
        load_w(Wxq, w_xq_d, 8, 1024, None, "Wxq")
        load_w(Wxo, w_xo_d, 8, 1024, None, "Wxo")
        for mb in range(2):
            xs, xk = xm[mb], ("xm", mb)
            dma(xs, mem_d[mb * 128:(mb + 1) * 128, :], [], [xk], xk)
            norm_transpose(xs, xk, xn[mb], ("xn", mb), junk, stats, memnT[:, :, mb * 128:(mb + 1) * 128], ("memnT", mb), "m", gain=(gM, "gM"))
        mk = [("memnT", 0), ("memnT", 1)]
        wk_ = wkeys("Wxkv", 8, 2048)
        for hj in range(8):
            pt, pk = psum()

            def mmk(e, pt=pt, hj=hj):
                for kc in range(8):
                    ins = e.matmul(pt[:, 0:256], Wxkv[:, kc, hj * 128:(hj + 1) * 128], memnT[:, kc, :], start=(kc == 0), stop=(kc == 7))
                return ins
            P.add("pe", mmk, r=mk + wk_, w=[pk])
            P.add("act", lambda e, pt=pt, hj=hj: e.copy(kxT[:, hj, :], pt[:, 0:256]), r=[pk], w=[("kxT", hj)])
        for mb in range(2):
            for half in range(2):
                pt, pk = psum()

                def mmv(e, pt=pt, mb=mb, half=half):
                    for kc in range(8):
                        ins = e.matmul(pt[:, :], memnT[:, kc, mb * 128:(mb + 1) * 128],
                                       Wxkv[:, kc, 1024 + half * 512:1024 + (half + 1) * 512], start=(kc == 0), stop=(kc == 7))
                    return ins
                P.add("pe", mmv, r=mk + wk_, w=[pk])
                P.add("dve", lambda e, pt=pt, mb=mb, half=half: e.tensor_copy(vx[:, mb, half * 512:(half + 1) * 512], pt[:, :]),
                      r=[pk], w=[("vx", mb, half)])
        yT_st = A.alloc([128, 8, 512], BF16)
        x1 = A.alloc([128, 4, D], F32)
        h1T = A.alloc([128, 8, 512], BF16)
        qxT = A.alloc([128, 8, 512], BF16)
        oxT = A.alloc([128, 8, 512], BF16)
        pX = [A.alloc([128, 512], BF16) for _ in range(4)]
        rsum = [A.alloc([128, 512], F32) for _ in range(2)]
        xscale = float(256 ** -0.5)
        xi = 0
        for st_i in range(NST):
            cs = slice(st_i * 512, (st_i + 1) * 512)
            dma(yT_st, yT_s[:, cs].rearrange("(c p) t -> p c t", p=128), ["yT_s"], ["yT_st"], "d_yT")
            for t in range(4):
                tile = st_i * 4 + t
                ts_ = slice(t * 128, (t + 1) * 128)
                xs, xk = xt[xi % 2], ("xt", xi % 2)
                xnb, xnk = xn[xi % 2], ("xn", xi % 2)
                xi += 1
                dma(xs, x_d[tile * 128:(tile + 1) * 128, :], [], [xk], xk)
                for half in range(2):
                    pt, pk = psum()
                    hs = slice(half * 512, (half + 1) * 512)

                    def mmo(e, pt=pt, ts_=ts_, hs=hs):
                        for kc in range(8):
                            ins = e.matmul(pt[:, :], yT_st[:, kc, ts_], Wo[:, kc, hs], start=(kc == 0), stop=(kc == 7))
                        return ins
                    P.add("pe", mmo, r=["yT_st"] + wkeys("Wo", 8, 1024), w=[pk])
                    P.add("dve", lambda e, pt=pt, t=t, hs=hs, xs=xs: e.tensor_tensor(x1[:, t, hs], pt[:, :], xs[:, hs], ALU.add),
                          r=[pk, xk], w=[("x1", t, half)])
            tps = []
            for t in range(4):
                ts_ = slice(t * 128, (t + 1) * 128)
                tps.append(norm_transpose(x1[:, t, :], [("x1", t, 0), ("x1", t, 1)], xn[t], ("xn", t), junk, stats,
                                          h1T[:, :, ts_], ("h1T", t), "d", col=t, gain=(gX, "gX"), defer=True))
            for tp_ in tps:
                tp_()
            hk = [("h1T", t) for t in range(4)]
            for hj in range(8):
                pt, pk = psum()

                def mmq(e, pt=pt, hj=hj):
                    for kc in range(8):
                        ins = e.matmul(pt[:, :], Wxq[:, kc, hj * 128:(hj + 1) * 128], h1T[:, kc, :], start=(kc == 0), stop=(kc == 7))
                    return ins
                P.add("pe", mmq, r=hk + wkeys("Wxq", 8, 1024), w=[pk])
                evac_d(qxT[:, hj, :], pt[:, :], [pk], [("qxT", hj)])
            def emit_S(h):
                for mb in range(2):
                    pt, pk = psum()

                    def mms(e, pt=pt, h=h, mb=mb):
                        for j in range(2):
                            ins = e.matmul(pt[:, :], kxT[:, h * 2 + j, mb * 128:(mb + 1) * 128], qxT[:, h * 2 + j, :],
                                           start=(j == 0), stop=(j == 1))
                        return ins
                    P.add("pe", mms, r=[("qxT", h * 2), ("qxT", h * 2 + 1)] + [("kxT", hj_) for hj_ in range(8)], w=[pk])
                    px_ = pX[(h % 2) * 2 + mb]
                    P.add("act", lambda e, pt=pt, px_=px_: e.activation(px_, pt[:, :], AF.Exp, scale=xscale),
                          r=[pk], w=[("pX", h % 2, mb)])

            def emit_rest(h):
                p0, p1 = pX[(h % 2) * 2], pX[(h % 2) * 2 + 1]
                pxk = [("pX", h % 2, 0), ("pX", h % 2, 1)]
                rs_ = rsum[h % 2]
                rk_ = ("rsum", h % 2)
                pt, pk = psum()

                def mmsum(e, pt=pt):
                    e.matmul(pt[:, :], onesb, p0, start=True, stop=False)
                    return e.matmul(pt[:, :], onesb, p1, start=False, stop=True)
                P.add("pe", mmsum, r=pxk + ["onesb"], w=[pk])
                P.add("dve", lambda e, pt=pt: e.reciprocal(rs_, pt[:, :]), r=[pk], w=[rk_])
                for j in range(2):
                    pt2, pk2 = psum()
                    hj = h * 2 + j

                    def mmov(e, pt2=pt2, hj=hj):
                        e.matmul(pt2[:, :], vx[:, 0, hj * 128:(hj + 1) * 128], p0, start=True, stop=False)
                        return e.matmul(pt2[:, :], vx[:, 1, hj * 128:(hj + 1) * 128], p1, start=False, stop=True)
                    P.add("pe", mmov, r=pxk + [("vx", mb_, hf_) for mb_ in range(2) for hf_ in range(2)], w=[pk2])
                    P.add("dve", lambda e, pt2=pt2, hj=hj: e.tensor_tensor(oxT[:, hj, :], pt2[:, :], rs_, ALU.mult),
                          r=[pk2, rk_], w=[("oxT", hj)])

            emit_S(0)
            for h in range(4):
                if h + 1 < 4:
                    emit_S(h + 1)
                emit_rest(h)
            ok_a = [("oxT", hj) for hj in range(6)]
            ok_b = [("oxT", 6), ("oxT", 7)]
            wxo_k = wkeys("Wxo", 8, 1024)
            grp = []
            for t in range(4):
                ts_ = slice(t * 128, (t + 1) * 128)
                for half in range(2):
                    pt, pk = psum()
                    hs = slice(half * 512, (half + 1) * 512)

                    def mmxo_a(e, pt=pt, ts_=ts_, hs=hs):
                        for hj in range(6):
                            ins = e.matmul(pt[:, :], oxT[:, hj, ts_], Wxo[:, hj, hs], start=(hj == 0), stop=False)
                        return ins
                    P.add("pe", mmxo_a, r=ok_a + wxo_k, w=[pk])
                    grp.append((t, half, pt, pk, ts_, hs))
            for (t, half, pt, pk, ts_, hs) in grp:
                tile = st_i * 4 + t

                def mmxo_b(e, pt=pt, ts_=ts_, hs=hs):
                    e.matmul(pt[:, :], oxT[:, 6, ts_], Wxo[:, 6, hs], start=False, stop=False)
                    return e.matmul(pt[:, :], oxT[:, 7, ts_], Wxo[:, 7, hs], start=False, stop=True)
                P.add("pe", mmxo_b, r=ok_b + wxo_k, w=[pk])
                P.add("dve", lambda e, pt=pt, t=t, hs=hs: e.tensor_tensor(x1[:, t, hs], pt[:, :], x1[:, t, hs], ALU.add),
                      r=[pk, ("x1", t, half)], w=[("x1", t, half)])
                if half == 1:
                    dma(x2_s[tile * 128:(tile + 1) * 128, :], x1[:, t, :], [("x1", t, 0), ("x1", t, 1)], ["x2_s"], "o_x2", q="pool")

    evd_rr = [0]

    def evac_d(out, in_, r, w):
        eng = "act"
        evd_rr[0] += 1
        if eng == "act":
            P.add("act", lambda e: e.copy(out, in_), r=r, w=w)
        else:
            P.add("dve", lambda e: e.tensor_copy(out, in_), r=r, w=w)

    def phase_E():
        Wgu = A.alloc([128, 8, 2 * DFF], BF16)
        Wd = A.alloc([128, 22, 1024], BF16)
        fgain = A.alloc([128, D], F32)
        dma(fgain, fnorm_d.partition_broadcast(128), [], ["fgain"], "e_fg")
        stats = A.alloc([128, 16], F32)
        gtile = A.alloc([128, D], F32)
        dma(gtile, ffn_norm_d.rearrange("(o d) -> o d", o=1).partition_broadcast(128), [], ["gtile"], "e_gt")
        GP = 1408
        for pi in (0, 2, 1, 3):
            for kc in range(8):
                dma(Wgu[:, kc, pi * GP:(pi + 1) * GP], w_gu_d[kc * 128:(kc + 1) * 128, pi * GP:(pi + 1) * GP],
                    [], [("Wgu", pi, kc)], ("wgu", pi), q="pool")
        for kc in range(22):
            dma(Wd[:, kc, :], w_down_d[kc * 128:(kc + 1) * 128, :], [], [("Wd", 0, kc)], "wd", q="pool")

        def gkeys(c0, c1):
            return [("Wgu", pi, kc) for pi in range(c0 // GP, (c1 - 1) // GP + 1) for kc in range(8)]
        xt = [A.alloc([128, D], F32) for _ in range(2)]
        xn = [A.alloc([128, D], BF16) for _ in range(4)]
        h2T = A.alloc([128, 8, 512], BF16)
        actT = A.alloc([128, 22, 512], BF16)
        gsb = [A.alloc([128, 512], F32) for _ in range(2)]
        xi = 0
        gi = 0
        xl = [A.alloc([128, D], F32) for _ in range(3)]
        li = [0]

        def load_norm(st_i):
            tps = []
            for t in range(4):
                tile = st_i * 4 + t
                xs, xk = xl[li[0] % 3], ("xl", li[0] % 3)
                xnb, xnk = xn[li[0] % 4], ("xn", li[0] % 4)
                li[0] += 1
                dma(xs, x2_s[tile * 128:(tile + 1) * 128, :], ["x2_s"], [xk], xk)
                tps.append(norm_transpose(xs, xk, xnb, xnk, xnb, stats, h2T[:, :, t * 128:(t + 1) * 128], ("h2T", t), "e", col=t,
                                          gain=(gtile, "gtile"), defer=True, junk_key=xnk))
            return tps

        for tp_ in load_norm(0):
            tp_()
        for st_i in range(NST):
            hk = [("h2T", t) for t in range(4)]
            next_tps = load_norm(st_i + 1) if st_i + 1 < NST else []
            for f in range(22):
                pg, pkg = psum()
                pu, pku = psum()

                def mmg(e, pg=pg, f=f):
                    for kc in range(8):
                        ins = e.matmul(pg[:, :], Wgu[:, kc, f * 128:(f + 1) * 128], h2T[:, kc, :], start=(kc == 0), stop=(kc == 7))
                    return ins

                def mmu(e, pu=pu, f=f):
                    for kc in range(8):
                        ins = e.matmul(pu[:, :], Wgu[:, kc, DFF + f * 128:DFF + (f + 1) * 128], h2T[:, kc, :],
                                       start=(kc == 0), stop=(kc == 7))
                    return ins
                P.add("pe", mmg, r=hk + gkeys(f * 128, (f + 1) * 128), w=[pkg])
                P.add("pe", mmu, r=hk + gkeys(DFF + f * 128, DFF + (f + 1) * 128), w=[pku])
                gb, gk_ = gsb[gi % 2], ("gsb", gi % 2)
                gi += 1
                P.add("act", lambda e, pg=pg, gb=gb: e.activation(gb, pg[:, :], AF.Silu), r=[pkg], w=[gk_])
                P.add("dve", lambda e, pu=pu, gb=gb, f=f: e.tensor_tensor(actT[:, f, :], pu[:, :], gb, ALU.mult),
                      r=[pku, gk_], w=[("actT", f)])
            ak = [("actT", f) for f in range(22)]
            for tp_ in next_tps:
                tp_()
            for t in range(4):
                tile = st_i * 4 + t
                ts_ = slice(t * 128, (t + 1) * 128)
                xs, xk = xt[xi % 2], ("xt", xi % 2)
                xi += 1
                dma(xs, x2_s[tile * 128:(tile + 1) * 128, :], ["x2_s"], [xk], xk)
                for half in range(2):
                    pt, pk = psum()
                    hs = slice(half * 512, (half + 1) * 512)

                    def mmd(e, pt=pt, ts_=ts_, hs=hs):
                        for f in range(22):
                            ins = e.matmul(pt[:, :], actT[:, f, ts_], Wd[:, f, hs], start=(f == 0), stop=(f == 21))
                        return ins
                    P.add("pe", mmd, r=ak + [("Wd", 0, kc) for kc in range(22)], w=[pk])
                    P.add("dve", lambda e, pt=pt, xs=xs, hs=hs: e.tensor_tensor(xs[:, hs], pt[:, :], xs[:, hs], ALU.add),
                          r=[pk, xk], w=[xk])
                P.add("act", lambda e, xs=xs: e.activation(gsb[0][:].bitcast(BF16), xs, AF.Square, accum_out=stats[:, 12:13]),
                      r=[xk], w=[("gsb", 0), "fss"])
                P.add("act", lambda e: e.activation(stats[:, 13:14], stats[:, 12:13], AF.Sqrt, bias=EPS, scale=1.0 / D),
                      r=["fss"], w=["fsd"])
                P.add("dve", lambda e: e.reciprocal(stats[:, 14:15], stats[:, 13:14]), r=["fsd"], w=["frs"])
                P.add("dve", lambda e, xs=xs: e.scalar_tensor_tensor(xs, xs, stats[:, 14:15], fgain, ALU.mult, ALU.mult),
                      r=[xk, "frs", "fgain"], w=[xk])
                dma(out_d[tile * 128:(tile + 1) * 128, :], xs, [xk], ["out"], "o_out", q="pool")

    phase_A()
    P.barrier()
    A.reset()
    if STOP >= 2:
        phase_B()
        P.barrier()
        A.reset()
    if STOP >= 3:
        phase_C()
        P.barrier()
        A.reset()
    if STOP >= 4:
        phase_D()
        P.barrier()
        A.reset()
    if STOP >= 5:
        phase_E()

    P.emit(nc, stack)
    return nc, P, stack


def _consts():
    idx = np.arange(128)
    uincl = (idx[:, None] <= idx[None, :]).astype(np.float32)
    inv = (10000.0 ** (-np.arange(0, 32, 2, dtype=np.float32) / 32)).astype(np.float32)
    invf = np.zeros((128, 1), np.float32)
    invf[:, 0] = inv[idx % 16]
    return {
        "c_identb": np.eye(128, dtype=np.float32).astype(ml_dtypes.bfloat16),
        "c_identf": np.eye(128, dtype=np.float32),
        "c_uincl": uincl,
        "c_uinclT": np.ascontiguousarray(uincl.T),
        "c_invf": invf,
    }


def make_in_maps(inputs):
    c = _consts()
    f = lambda a: np.ascontiguousarray(np.asarray(a))
    shared = {
        "attn_norm": f(inputs["attn_norm"][0]), "w_in": f(inputs["w_in"][0]),
        "q_norm": f(inputs["q_norm"][0]), "w_uq": f(inputs["w_uq"][0]),
        "kv_norm": f(inputs["kv_norm"][0]), "w_ukv": f(inputs["w_ukv"][0]),
        "mlstm_conv": f(inputs["mlstm_conv"][0]), "mlstm_gate_bias": f(inputs["mlstm_gate_bias"]),
        "mlstm_norm": f(inputs["mlstm_norm"]), "w_out": f(inputs["w_out"][0]),
        "xattn_norm": f(inputs["xattn_norm"][0]), "mem_norm": f(inputs["mem_norm"][0]),
        "w_xq": f(inputs["w_xq"][0]), "w_xkv": f(inputs["w_xkv"][0]), "w_xo": f(inputs["w_xo"][0]),
        "ffn_norm": f(inputs["ffn_norm"][0]), "w_gate_up": f(inputs["w_gate_up"][0]),
        "w_down": f(inputs["w_down"][0]), "final_norm": f(np.asarray(inputs["final_norm"]).reshape(1, D)),
    }
    shared.update(c)
    maps = []
    for b in range(8):
        m = dict(shared)
        m["x"] = f(inputs["x"][b])
        m["mem"] = f(inputs["mem"][b])
        m["positions"] = f(np.asarray(inputs["positions"][b]).reshape(1, S).astype(np.int32))
        maps.append(m)
    return maps


def kernel(**inputs):
    nc, P, stack = build_nc()
    with stack:
        res = run_bass_kernel_spmd(nc, make_in_maps(inputs), core_ids=list(range(8)))
    out = np.stack([np.asarray(r["out"]) for r in res.results], axis=0)
    return out.astype(np.float32)
```

```python
import numpy as np
import ml_dtypes
import concourse.bass as bass
import concourse.mybir as mybir
from concourse.bass_utils import run_bass_kernel_spmd

F32 = mybir.dt.float32
BF16 = mybir.dt.bfloat16
I32 = mybir.dt.int32
U8 = mybir.dt.uint8
AF = mybir.ActivationFunctionType
ALU = mybir.AluOpType
AX = mybir.AxisListType

S = 4096
D = 1024
NT = 32
NST = 8
EPS = 1e-6
INW = 2480
DFF = 2816
NMEM = 256
TWO_PI = 2.0 * np.pi
C1 = 6.28125
C2 = TWO_PI - 6.28125


class Prog:
    COMPUTE = ("pe", "act", "dve", "pool")

    def __init__(self):
        self.ops = []
        self.lastw = {}
        self.rd_eng = {}
        self.rd_dma = {}
        self.last_on = {}
        self.dma_keys = []

    def add(self, eng, fn, r=(), w=(), dma=None):
        idx = len(self.ops)
        deps = set()
        for k in r:
            if k in self.lastw:
                deps.add(self.lastw[k])
        for k in w:
            if k in self.lastw:
                deps.add(self.lastw[k])
            deps.update(self.rd_eng.get(k, {}).values())
            deps.update(self.rd_dma.get(k, ()))
        for k in w:
            self.lastw[k] = idx
            self.rd_eng[k] = {}
            self.rd_dma[k] = []
        for k in r:
            if k in w:
                continue
            if dma is not None:
                self.rd_dma.setdefault(k, []).append(idx)
            else:
                self.rd_eng.setdefault(k, {})[eng] = idx
        deps.discard(idx)
        if dma is not None and dma not in self.dma_keys:
            self.dma_keys.append(dma)
        self.ops.append(dict(eng=eng, fn=fn, dma=dma, deps=deps, idx=idx))
        self.last_on[(eng, dma)] = idx
        return idx

    def barrier(self):
        lasts = set(self.last_on.values())
        for eng in ("pe", "act", "dve", "pool", "sp"):
            idx = len(self.ops)
            self.ops.append(dict(eng=eng, fn=None, dma=None, deps=set(lasts), idx=idx))
            self.last_on[(eng, None)] = idx
        self.lastw = {}
        self.rd_eng = {}
        self.rd_dma = {}

    def emit(self, nc, stack):
        ops = self.ops
        need = [False] * len(ops)
        for op in ops:
            for d in op["deps"]:
                p = ops[d]
                if p["dma"] is None and p["eng"] == "pe" and op["eng"] == "pe":
                    continue
                need[d] = True
        sems = {}
        for e in self.COMPUTE:
            sems[e] = stack.enter_context(nc.semaphore("s_" + e))
        dsem = {}
        for k in self.dma_keys:
            dsem[k] = stack.enter_context(nc.semaphore("d_" + str(k)))
        cnt = {e: 0 for e in self.COMPUTE}
        dcnt = {k: 0 for k in self.dma_keys}
        plan = {e: [] for e in ("pe", "act", "dve", "pool", "sp")}
        waited = {e: {} for e in plan}
        for op in ops:
            e = op["eng"]
            waits = {}
            for d in op["deps"]:
                p = ops[d]
                if p["fn"] is None:
                    continue
                if p["dma"] is not None:
                    key = ("d", p["dma"])
                    val = dcnt[p["dma"]]
                else:
                    if p["eng"] == "pe" and e == "pe":
                        continue
                    key = ("c", p["eng"])
                    val = p["val"]
                if waits.get(key, 0) < val:
                    waits[key] = val
            wl = []
            for key, val in waits.items():
                if waited[e].get(key, 0) >= val:
                    continue
                waited[e][key] = val
                wl.append((dsem[key[1]] if key[0] == "d" else sems[key[1]], val))
            sig = None
            if op["fn"] is not None:
                if op["dma"] is not None:
                    dcnt[op["dma"]] += 16
                    sig = (dsem[op["dma"]], 16)
                elif need[op["idx"]]:
                    cnt[e] += 1
                    op["val"] = cnt[e]
                    sig = (sems[e], 1)
                else:
                    op["val"] = cnt[e]
            plan[e].append((wl, op["fn"], sig))
        fin = []
        for e in self.COMPUTE:
            if cnt[e] > 0:
                fin.append((sems[e], cnt[e]))
        for k in self.dma_keys:
            fin.append((dsem[k], dcnt[k]))
        self.stats = dict(n_ops=len(ops), cnt=dict(cnt), n_dma_keys=len(self.dma_keys))

        def run(e_name, eng):
            for wl, fn, sig in plan[e_name]:
                for s, v in wl:
                    eng.wait_ge(s, v)
                if fn is None:
                    continue
                ins = fn(eng)
                if sig is not None:
                    ins.then_inc(sig[0], sig[1])
            if e_name == "sp":
                for s, v in fin:
                    eng.wait_ge(s, v)

        with nc.Block() as block:
            @block.tensor
            def _(eng):
                run("pe", eng)

            @block.scalar
            def _(eng):
                run("act", eng)

            @block.vector
            def _(eng):
                run("dve", eng)

            @block.gpsimd
            def _(eng):
                run("pool", eng)

            @block.sync
            def _(eng):
                run("sp", eng)


class Arena:
    def __init__(self, nc, nbytes):
        self.t = nc.alloc_sbuf_tensor("arena", [128, nbytes], U8)
        self.nbytes = nbytes
        self.off = 0
        self.mark = 0

    def alloc(self, shape, dtype):
        esz = {F32: 4, BF16: 2, I32: 4}[dtype]
        n = 1
        for s_ in shape[1:]:
            n *= s_
        nb = (n * esz + 31) // 32 * 32
        assert self.off + nb <= self.nbytes, ("SBUF arena overflow", self.off, nb)
        v = self.t[0:shape[0], self.off:self.off + nb].bitcast(dtype)[:, 0:n]
        self.off += nb
        if len(shape) == 3:
            v = v.rearrange("p (a b) -> p a b", a=shape[1])
        elif len(shape) == 4:
            v = v.rearrange("p (a b c) -> p a b c", a=shape[1], b=shape[2])
        return v

    def view_at(self, off, shape, dtype):
        save = self.off
        self.off = off
        v = self.alloc(shape, dtype)
        self.off = save
        return v

    def set_mark(self):
        self.mark = self.off

    def reset(self):
        self.off = self.mark


def build_nc(debug=None, STOP=9):
    nc = bass.Bass("TRN2", target_bir_lowering=False)
    import contextlib
    stack = contextlib.ExitStack()
    P = Prog()

    def din(name, shape, dt=F32):
        return nc.dram_tensor(name, list(shape), dt, kind="ExternalInput").ap()

    def dscr(name, shape, dt):
        kind = "ExternalOutput" if (debug and name in debug) else "Internal"
        return nc.dram_tensor(name, list(shape), dt, kind=kind).ap()

    x_d = din("x", [S, D])
    mem_d = din("mem", [NMEM, D])
    pos_d = din("positions", [1, S], I32)
    attn_norm_d = din("attn_norm", [D])
    w_in_d = din("w_in", [D, INW])
    q_norm_d = din("q_norm", [256])
    w_uq_d = din("w_uq", [256, 768])
    kv_norm_d = din("kv_norm", [128])
    w_ukv_d = din("w_ukv", [128, 1024])
    conv_d = din("mlstm_conv", [3, 1024])
    gbias_d = din("mlstm_gate_bias", [1, 16])
    mnorm_d = din("mlstm_norm", [1, 512])
    w_out_d = din("w_out", [D, D])
    xattn_norm_d = din("xattn_norm", [D])
    mem_norm_d = din("mem_norm", [D])
    w_xq_d = din("w_xq", [D, D])
    w_xkv_d = din("w_xkv", [D, 2 * D])
    w_xo_d = din("w_xo", [D, D])
    ffn_norm_d = din("ffn_norm", [D])
    w_gu_d = din("w_gate_up", [D, 2 * DFF])
    w_down_d = din("w_down", [DFF, D])
    fnorm_d = din("final_norm", [1, D])
    identb_d = din("c_identb", [128, 128], BF16)
    identf_d = din("c_identf", [128, 128])
    uincl_d = din("c_uincl", [128, 128])
    uinclT_d = din("c_uinclT", [128, 128])
    invf_d = din("c_invf", [128, 1])
    out_d = nc.dram_tensor("out", [S, D], F32, kind="ExternalOutput").ap()

    qkT_s = dscr("qkT_s", [16, 96, S], BF16)
    v_s = dscr("v_s", [S, 8 * 128], BF16)
    qkmT_s = dscr("qkmT_s", [8, 128, S], BF16)
    vml_s = dscr("vml_s", [S, 4 * 129], BF16)
    o_s = dscr("o_s", [S, 512], F32)
    yT_s = dscr("yT_s", [1024, S], BF16)
    x2_s = dscr("x2_s", [S, D], F32)
    gates_s = dscr("gates_s", [128, 32 * 16], F32)
    tab_s = dscr("tab_s", [64, S], F32)

    A = Arena(nc, 211968)
    psbig = nc.alloc_psum_tensor("psbig", [128, 4096], F32)
    ps = [psbig[:, i * 512:(i + 1) * 512] for i in range(8)]
    ps_rr = [0]

    def psum():
        i = ps_rr[0] % 8
        ps_rr[0] += 1
        return ps[i], ("ps", i)

    dma_rr = [0]

    def dma(out, in_, r, w, key, q="sp", slow=False):
        def fn(e, out=out, in_=in_):
            if slow:
                return e.dma_start(out=out, in_=in_, allow_slow_non_contiguous=True)
            return e.dma_start(out=out, in_=in_)
        return P.add(q, fn, r=r, w=w, dma=key)

    identb = A.alloc([128, 128], BF16)
    identf = A.alloc([128, 128], F32)
    uincl = A.alloc([128, 128], F32)
    uinclT = A.alloc([128, 128], F32)
    onesf = A.alloc([128, 128], F32)
    maskf = A.alloc([128, 128], BF16)
    maskb = A.alloc([128, 128], BF16)
    invf = A.alloc([128, 1], F32)
    gates_all = A.alloc([128, 32, 16], F32)
    dma(identb, identb_d, [], ["identb"], "c0")
    dma(identf, identf_d, [], ["identf"], "c1")
    dma(uincl, uincl_d, [], ["uincl"], "c2")
    dma(uinclT, uinclT_d, [], ["uinclT"], "c3")
    dma(invf, invf_d, [], ["invf"], "c4")
    P.add("pool", lambda e: e.memset(onesf, 1.0), w=["onesf"])
    P.add("dve", lambda e: e.tensor_copy(maskf, uincl), r=["uincl"], w=["maskf"])
    P.add("dve", lambda e: e.tensor_copy(maskb, uinclT), r=["uinclT"], w=["maskb"])
    A.set_mark()

    def gain_cols(src_d, n, name):
        t = A.alloc([128, n], F32)
        dma(t, src_d.rearrange("(c p) -> p c", p=128), [], [name], "g_" + name, slow=True)
        return t

    def rstd_from_ss(ss, sd, rstd, n, key):
        P.add("act", lambda e: e.activation(sd, ss, AF.Sqrt, bias=EPS, scale=1.0 / n),
              r=[key + "ss"], w=[key + "sd"])
        P.add("dve", lambda e: e.reciprocal(rstd, sd), r=[key + "sd"], w=[key + "rstd"])

    def phase_A():
        Wb = A.alloc([128, 8, 2560], BF16)
        xt = [A.alloc([128, D], F32) for _ in range(3)]
        wq = A.alloc([128, 2, 8, 128], BF16)
        wk = A.alloc([128, 8, 96], BF16)
        wv = A.alloc([128, 512], BF16)
        gA = gain_tile(attn_norm_d, D, "gA")
        g_q = gain_cols(q_norm_d, 2, "qn")
        g_kv = gain_cols(kv_norm_d, 1, "kvn")
        gbias = A.alloc([128, 16], F32)
        dma(gbias, gbias_d.partition_broadcast(128), [], ["gbias"], "c5")
        cos2T = A.alloc([32, S], F32)
        sin2T = A.alloc([32, S], F32)
        junk = A.alloc([128, D], BF16)
        xn = [A.alloc([128, D], BF16) for _ in range(4)]
        hT = A.alloc([128, 8, 512], BF16)
        cqn = A.alloc([128, 384], BF16)
        cqnT = A.alloc([128, 3, 512], BF16)
        qk_st = A.alloc([128, 16, 512], BF16)
        krT = A.alloc([32, 512], BF16)
        v_st = A.alloc([128, 4, 8, 128], BF16)
        zext = A.alloc([128, 8, 514], F32)
        qkm_st = A.alloc([128, 8, 512], BF16)
        wconv = A.alloc([128, 3, 8], F32)
        fin0 = A.alloc([128, 8], F32)
        fin1 = A.alloc([128, 8], F32)
        for tap in range(3):
            dma(wconv[:, tap, :], conv_d[tap].rearrange("(c p) -> p c", p=128), [], ["wconv"], "c_wconv", slow=True)
        P.add("pool", lambda e: e.memset(zext[:, :, 0:2], 0.0), w=[("zext", j) for j in range(8)])
        vml_st = A.alloc([128, 4, 4, 129], BF16)
        o_st = A.alloc([128, 4, 512], F32)
        stats = A.alloc([128, 24], F32)
        tA = A.alloc([32, 512], F32)
        tB = A.alloc([32, 512], F32)

        load_w(Wb, w_in_d, 8, INW, None, "Wb")
        n = 0
        for kc in range(8):
            P.add("pool", lambda e, kc=kc: e.tensor_scalar(
                Wb[:, kc, 2480:2496], Wb[:, kc, 400:416], -1.0, None, ALU.mult),
                r=[("Wb", 0, kc)], w=[("Wbr", kc)])
            P.add("pool", lambda e, kc=kc: e.tensor_copy(Wb[:, kc, 2496:2512], Wb[:, kc, 384:400]),
                  r=[("Wb", 0, kc)], w=[("Wbr2", kc)])
        for kc in range(2):
            st = xt[n % 2]
            dma(st[:, 0:768], w_uq_d[kc * 128:(kc + 1) * 128, :], [], [("xt", n % 2)], ("xt", n % 2))
            sv = st[:, 0:768].rearrange("p (h c) -> p h c", c=96)
            g = g_q[:, kc:kc + 1]
            rk = [("xt", n % 2), "qn"]
            P.add("dve", lambda e, sv=sv, kc=kc, g=g: e.tensor_scalar(
                wq[:, kc, :, 0:32], sv[:, :, 64:96], g, None, ALU.mult), r=rk, w=[("wq", kc, 0)])
            P.add("pool", lambda e, sv=sv, kc=kc, g=g: e.tensor_scalar(
                wq[:, kc, :, 32:96], sv[:, :, 0:64], g, None, ALU.mult), r=rk, w=[("wq", kc, 1)])
            P.add("dve", lambda e, sv=sv, kc=kc, g=g: e.tensor_scalar(
                wq[:, kc, :, 96:112], sv[:, :, 80:96], g, -1.0, ALU.mult, ALU.mult), r=rk, w=[("wq", kc, 2)])
            P.add("pool", lambda e, sv=sv, kc=kc, g=g: e.tensor_scalar(
                wq[:, kc, :, 112:128], sv[:, :, 64:80], g, None, ALU.mult), r=rk, w=[("wq", kc, 3)])
            n += 1
        st = xt[n % 2]
        dma(st[:, 0:1024], w_ukv_d[:, :], [], [("xt", n % 2)], ("xt", n % 2))
        sv = st[:, 0:1024].rearrange("p (h c) -> p h c", c=128)
        rk = [("xt", n % 2), "kvn"]
        P.add("pool", lambda e: e.memset(wk[:, :, 0:32], 0.0), w=["wk0"])
        P.add("dve", lambda e, sv=sv: e.tensor_scalar(
            wk[:, :, 32:96], sv[:, :, 0:64], g_kv[:, 0:1], None, ALU.mult), r=rk, w=["wk1"])
        P.add("pool", lambda e, sv=sv: e.tensor_scalar(
            wv.rearrange("p (h c) -> p h c", c=64), sv[:, :, 64:128], g_kv[:, 0:1], None, ALU.mult),
            r=rk, w=["wv"])
        n += 1
        P.add("pool", lambda e: e.memset(v_st[:, :, :, 64:128], 1.0), w=["v_ones"])
        P.add("pool", lambda e: e.memset(vml_st[:, :, :, 128:129], 1.0), w=["vml_ones"])

        off_tabs = A.off
        posi = A.alloc([32, 1024], I32)
        t0 = A.alloc([32, 1024], F32)
        t1 = A.alloc([32, 1024], F32)
        t2 = A.alloc([32, 1024], F32)
        ki = A.alloc([32, 1024], I32)
        for c in range(4):
            cs = slice(c * 1024, (c + 1) * 1024)
            dma(posi, pos_d[0:1, cs].partition_broadcast(32), [], ["posi"], "posi")
            P.add("dve", lambda e: e.tensor_copy(t0, posi), r=["posi"], w=["t0"])
            P.add("dve", lambda e: e.tensor_scalar(t0, t0, invf[0:32, 0:1], None, ALU.mult),
                  r=["t0", "invf"], w=["t0"])
            P.add("dve", lambda e: e.tensor_scalar(ki, t0, 1.0 / TWO_PI, None, ALU.mult),
                  r=["t0"], w=["ki"])
            P.add("dve", lambda e: e.tensor_copy(t1, ki), r=["ki"], w=["t1"])
            P.add("dve", lambda e: e.scalar_tensor_tensor(t0, t1, -C1, t0, ALU.mult, ALU.add),
                  r=["t0", "t1"], w=["t0"])
            P.add("dve", lambda e: e.scalar_tensor_tensor(t0, t1, -C2, t0, ALU.mult, ALU.add),
                  r=["t0", "t1"], w=["t0"])

            def fold(dst, key):
                P.add("dve", lambda e: e.tensor_scalar(t1, dst, float(np.pi), -TWO_PI, ALU.is_gt, ALU.mult),
                      r=[key], w=["t1"])
                P.add("dve", lambda e: e.tensor_tensor(dst, dst, t1, ALU.add), r=[key, "t1"], w=[key])
                P.add("dve", lambda e: e.tensor_scalar(dst, dst, float(-np.pi), float(np.pi), ALU.max, ALU.min),
                      r=[key], w=[key])
            fold(t0, "t0")
            P.add("act", lambda e, cs=cs: e.activation(sin2T[:, cs], t0, AF.Sin), r=["t0"], w=["sin2T"])
            P.add("dve", lambda e: e.tensor_scalar(t2, t0, float(np.pi / 2), None, ALU.add),
                  r=["t0"], w=["t2"])
            fold(t2, "t2")
            P.add("act", lambda e, cs=cs: e.activation(cos2T[:, cs], t2, AF.Sin), r=["t2"], w=["cos2T"])

        cacc = A.view_at(off_tabs, [128, 8, 512], F32)
        P.add("pool", lambda e: e.memset(cacc, 0.0), r=["posi", "t0", "t1", "t2", "ki"],
              w=["posi", "t0", "t1", "t2", "ki"] + [("cacc", j) for j in range(8)])
        xi = 0
        evac_rr = [0]

        def evac(out, in_, r, w):
            eng = "dve" if evac_rr[0] % 3 == 2 else "act"
            evac_rr[0] += 1
            if eng == "act":
                P.add("act", lambda e: e.copy(out, in_), r=r, w=w)
            else:
                P.add("dve", lambda e: e.tensor_copy(out, in_), r=r, w=w)

        xi_box = [0]

        def x_chains(st_i):
            tparts = []
            for t in range(4):
                tile = st_i * 4 + t
                xs = xt[xi_box[0] % 3]
                xk = ("xt", xi_box[0] % 3)
                xnb = xn[xi_box[0] % 4]
                xnk = ("xn", xi_box[0] % 4)
                xi_box[0] += 1
                dma(xs, x_d[tile * 128:(tile + 1) * 128, :], [], [xk], xk)
                ssc = stats[:, 0:1]
                sc = (0, 10, 16, 19)[t]
                xp = "x%d" % t
                P.add("act", lambda e, xs=xs, sc=sc: e.activation(junk, xs, AF.Square, accum_out=stats[:, sc:sc + 1]),
                      r=[xk], w=["junk", xp + "ss"])
                rstd_from_ss(stats[:, sc:sc + 1], stats[:, sc + 1:sc + 2], stats[:, sc + 2:sc + 3], D, xp)
                P.add("dve", lambda e, xs=xs, xnb=xnb, sc=sc: e.scalar_tensor_tensor(xnb, xs, stats[:, sc + 2:sc + 3], gA, ALU.mult, ALU.mult),
                      r=[xk, xp + "rstd", "gA"], w=[xnk])
                def tpart(xnb=xnb, xnk=xnk, t=t):
                    pt, pk = psum()
                    ptv = pt[:].bitcast(BF16).rearrange("p (a b) -> p a b", a=8)

                    def tr(e, xnb=xnb, ptv=ptv):
                        for kc in range(8):
                            ins = e.transpose(ptv[:, kc, :], xnb[:, kc * 128:(kc + 1) * 128], identb)
                        return ins
                    P.add("pe", tr, r=[xnk, "identb"], w=[pk])
                    evac(hT[:, :, t * 128:(t + 1) * 128], ptv, [pk], [("hT", t)])
                tparts.append(tpart)
            return tparts

        for tp_ in x_chains(0):
            tp_()
        for st_i in range(NST):
            tok0 = st_i * 512
            hk = [("hT", t) for t in range(4)]
            wbk = wkeys("Wb", 8, INW)
            for t in range(4):
                tile = st_i * 4 + t
                ts_ = slice(t * 128, (t + 1) * 128)
                pt, pk = psum()

                def mm1(e, pt=pt, ts_=ts_):
                    for kc in range(8):
                        ins = e.matmul(pt[:, 0:384], hT[:, kc, ts_], Wb[:, kc, 0:384], start=(kc == 0), stop=(kc == 7))
                    return ins
                P.add("pe", mm1, r=[("hT", t)] + wbk, w=[pk])
                P.add("act", lambda e, pt=pt: e.activation(junk[:, 0:256], pt[:, 0:256], AF.Square, accum_out=stats[:, 4:5]),
                      r=[pk], w=["junk", "css"])
                P.add("act", lambda e, pt=pt: e.activation(junk[:, 256:384], pt[:, 256:384], AF.Square, accum_out=stats[:, 5:6]),
                      r=[pk, "css"], w=["junk", "css"])
                P.add("act", lambda e: e.activation(stats[:, 6:7], stats[:, 4:5], AF.Sqrt, bias=EPS, scale=1.0 / 256),
                      r=["css"], w=["csd"])
                P.add("act", lambda e: e.activation(stats[:, 7:8], stats[:, 5:6], AF.Sqrt, bias=EPS, scale=1.0 / 128),
                      r=["css", "csd"], w=["csd"])
                P.add("dve", lambda e: e.reciprocal(stats[:, 8:10], stats[:, 6:8]), r=["csd"], w=["crstd"])
                P.add("dve", lambda e, pt=pt: e.tensor_scalar(cqn[:, 0:256], pt[:, 0:256], stats[:, 8:9], None, ALU.mult),
                      r=[pk, "crstd"], w=["cqn0"])
                P.add("dve", lambda e, pt=pt: e.tensor_scalar(cqn[:, 256:384], pt[:, 256:384], stats[:, 9:10], None, ALU.mult),
                      r=[pk, "crstd"], w=["cqn1"])
                for (c0, kind) in ((1440, "v"), (1952, "o")):
                    pt4, pk4 = psum()

                    def mm4(e, pt4=pt4, ts_=ts_, c0=c0):
                        for kc in range(8):
                            ins = e.matmul(pt4[:, :], hT[:, kc, ts_], Wb[:, kc, c0:c0 + 512], start=(kc == 0), stop=(kc == 7))
                        return ins
                    P.add("pe", mm4, r=[("hT", t)] + wbk, w=[pk4])
                    if kind == "v":
                        evac(vml_st[:, t, :, 0:128], pt4[:, :].rearrange("p (h c) -> p h c", c=128),
                             [pk4, "vml_ones"], [("vml_st", t)])
                    else:
                        P.add("act", lambda e, pt4=pt4, t=t: e.activation(o_st[:, t, :], pt4[:, :], AF.Sigmoid),
                              r=[pk4], w=[("o_st", t)])
                pt5, pk5 = psum()

                def mm5(e, pt5=pt5, ts_=ts_):
                    for kc in range(8):
                        ins = e.matmul(pt5[:, 0:16], hT[:, kc, ts_], Wb[:, kc, 2464:2480], start=(kc == 0), stop=(kc == 7))
                    return ins
                P.add("pe", mm5, r=[("hT", t)] + wbk, w=[pk5])
                P.add("dve", lambda e, pt5=pt5, tile=tile: e.tensor_tensor(gates_all[:, tile, :], pt5[:, 0:16], gbias, ALU.add),
                      r=[pk5, "gbias"], w=[("gates", tile)])
                pt2, pk2 = psum()
                ptv2 = pt2[:].bitcast(BF16).rearrange("p (a b) -> p a b", a=8)

                def tr2(e, ptv2=ptv2):
                    for j in range(3):
                        ins = e.transpose(ptv2[:, j, :], cqn[:, j * 128:(j + 1) * 128], identb)
                    return ins
                P.add("pe", tr2, r=["cqn0", "cqn1", "identb"], w=[pk2])
                evac(cqnT[:, :, ts_], ptv2[:, 0:3, :], [pk2], [("cqnT", t)])
                pt3, pk3 = psum()
                P.add("pe", lambda e, pt3=pt3, ts_=ts_: e.matmul(pt3[:, :], cqnT[:, 2, ts_], wv, start=True, stop=True),
                      r=[("cqnT", t), "wv"], w=[pk3])
                evac(v_st[:, t, :, 0:64], pt3[:, :].rearrange("p (h c) -> p h c", c=64), [pk3, "v_ones"], [("v_st", t)])
            cs = slice(tok0, tok0 + 512)
            cqk = [("cqnT", t) for t in range(4)]
            pa, pka = psum()
            pb, pkb = psum()

            def mmkr(e, pa=pa, c0=384):
                for kc in range(8):
                    ins = e.matmul(pa[0:32, :], Wb[:, kc, 384:416], hT[:, kc, :], start=(kc == 0), stop=(kc == 7))
                return ins

            def mmkrr(e, pb=pb):
                for kc in range(8):
                    ins = e.matmul(pb[0:32, :], Wb[:, kc, 2480:2512], hT[:, kc, :], start=(kc == 0), stop=(kc == 7))
                return ins
            P.add("pe", mmkr, r=hk + wbk, w=[pka])
            P.add("pe", mmkrr, r=hk + [("Wbr", kc) for kc in range(8)] + [("Wbr2", kc) for kc in range(8)], w=[pkb])

            def rope(pa, pka, pb, pkb, dst, dkey, cs=cs):
                P.add("dve", lambda e: e.tensor_tensor(tA, pa[0:32, :], cos2T[:, cs], ALU.mult),
                      r=[pka, "cos2T"], w=["tA"])
                P.add("dve", lambda e: e.tensor_tensor(tB, pb[0:32, :], sin2T[:, cs], ALU.mult),
                      r=[pkb, "sin2T"], w=["tB"])
                P.add("pool", lambda e: e.tensor_tensor(dst, tA, tB, ALU.add), r=["tA", "tB"], w=[dkey])
            rope(pa, pka, pb, pkb, krT, "krT")
            for h in range(8):
                pq, pkq = psum()
                pr, pkr = psum()

                def mmq(e, pq=pq, h=h):
                    for kc in range(2):
                        ins = e.matmul(pq[0:96, :], wq[:, kc, h, 0:96], cqnT[:, kc, :], start=(kc == 0), stop=(kc == 1))
                    return ins

                def mmr(e, pr=pr, h=h):
                    for kc in range(2):
                        ins = e.matmul(pr[0:32, :], wq[:, kc, h, 96:128], cqnT[:, kc, :], start=(kc == 0), stop=(kc == 1))
                    return ins
                wqk = [("wq", kc, j) for kc in range(2) for j in range(4)]
                P.add("pe", mmq, r=cqk + wqk, w=[pkq])
                P.add("pe", mmr, r=cqk + wqk, w=[pkr])
                P.add("act", lambda e, pq=pq, h=h: e.copy(qk_st[0:96, h, :], pq[0:96, :]),
                      r=[pkq], w=[("qk_st", h)])
                rope(pq, pkq, pr, pkr, qk_st[0:32, h, :], ("qk_st", h))
                pk_, pkk = psum()
                P.add("pe", lambda e, pk_=pk_, h=h: e.matmul(pk_[0:96, :], wk[:, h, :], cqnT[:, 2, :], start=True, stop=True),
                      r=cqk + ["wk0", "wk1"], w=[pkk])
                P.add("act", lambda e, pk_=pk_, h=h: e.copy(qk_st[0:96, 8 + h, :], pk_[0:96, :]),
                      r=[pkk], w=[("qk_st", 8 + h)])
                P.add("pool", lambda e, h=h: e.tensor_copy(qk_st[0:32, 8 + h, :], krT),
                      r=["krT"], w=[("qk_st", 8 + h)])
            dma(qkT_s[:, :, cs].rearrange("j p t -> p j t"), qk_st[0:96, :, :],
                [("qk_st", j) for j in range(16)], ["qkT_s"], "o_qk", q="pool")
            next_tp = x_chains(st_i + 1) if st_i + 1 < NST else []
            for j in range(8):
                pz, pkz = psum()

                def mmz(e, pz=pz, j=j):
                    for kc in range(8):
                        ins = e.matmul(pz[:, :], Wb[:, kc, 416 + j * 128:416 + (j + 1) * 128], hT[:, kc, :],
                                       start=(kc == 0), stop=(kc == 7))
                    return ins
                P.add("pe", mmz, r=hk + wbk, w=[pkz])
                evac(zext[:, j, 2:514], pz[:, :], [pkz], [("zext", j)])
                P.add("act", lambda e, j=j: e.activation(cacc[:, j, :], zext[:, j, 1:513], AF.Copy, scale=wconv[:, 1, j:j + 1]),
                      r=[("zext", j), "wconv", "qkm_st"], w=[("cacc", j)])
                P.add("dve", lambda e, j=j: e.scalar_tensor_tensor(cacc[:, j, :], zext[:, j, 0:512], wconv[:, 0, j:j + 1], cacc[:, j, :],
                                                                     ALU.mult, ALU.add), r=[("zext", j), "wconv"], w=[("cacc", j)])
                P.add("dve", lambda e, j=j: e.scalar_tensor_tensor(cacc[:, j, :], zext[:, j, 2:514], wconv[:, 2, j:j + 1], cacc[:, j, :],
                                                                     ALU.mult, ALU.add), r=[("zext", j), "wconv"], w=[("cacc", j)])
            ck = [("cacc", j) for j in range(8)]
            zk_ = [("zext", j) for j in range(8)]
            P.add("act", lambda e: e.activation(qkm_st[:].rearrange("p a b -> p (a b)"), cacc[:].rearrange("p a b -> p (a b)"), AF.Silu),
                  r=ck, w=["qkm_st"])
            if st_i == 0:
                dma(qkmT_s[:, :, 0:511].rearrange("j p t -> p j t"), qkm_st[:, :, 1:512], ["qkm_st"], ["qkmT_s"], "o_zqk", q="pool")
            else:
                dma(qkmT_s[:, :, tok0 - 1:tok0 + 511].rearrange("j p t -> p j t"), qkm_st[:, :, 0:512], ["qkm_st"], ["qkmT_s"],
                    "o_zqk", q="pool")
            P.add("pool", lambda e: e.tensor_copy(zext[:, :, 0:2], zext[:, :, 512:514]), r=zk_, w=zk_)
            for tp_ in next_tp:
                tp_()
            dma(v_s[tok0:tok0 + 512, :].rearrange("(a p) c -> p a c", p=128),
                v_st[:].rearrange("p a h c -> p a (h c)"),
                [("v_st", t) for t in range(4)], ["v_s"], "o_v", q="pool")
            dma(vml_s[tok0:tok0 + 512, :].rearrange("(a p) c -> p a c", p=128),
                vml_st[:].rearrange("p a h c -> p a (h c)"),
                [("vml_st", t) for t in range(4)], ["vml_s"], "o_vml", q="pool")
            dma(o_s[tok0:tok0 + 512, :].rearrange("(a p) c -> p a c", p=128), o_st,
                [("o_st", t) for t in range(4)], ["o_s"], "o_o", q="pool")
        zk_ = [("zext", j) for j in range(8)]
        P.add("dve", lambda e: e.tensor_tensor(fin0, zext[:, :, 0], wconv[:, 0, :], ALU.mult), r=zk_ + ["wconv"], w=["fin0"])
        P.add("dve", lambda e: e.tensor_tensor(fin1, zext[:, :, 1], wconv[:, 1, :], ALU.mult), r=zk_ + ["wconv"], w=["fin1"])
        P.add("dve", lambda e: e.tensor_tensor(fin0, fin0, fin1, ALU.add), r=["fin0", "fin1"], w=["fin0"])
        P.add("act", lambda e: e.activation(qkm_st[:, :, 0], fin0, AF.Silu), r=["fin0", "qkm_st"], w=["qkm_st"])
        dma(qkmT_s[:, :, S - 1:S].rearrange("j p t -> p j t"), qkm_st[:, :, 0:1], ["qkm_st"], ["qkmT_s"], "o_zqk", q="pool", slow=True)
        if debug and "tab_s" in debug:
            dma(tab_s[0:32, :], cos2T, ["cos2T"], ["tab_s0"], "o_t0")
            dma(tab_s[32:64, :], sin2T, ["sin2T"], ["tab_s1"], "o_t1")
        if debug and "gates_s" in debug:
            dma(gates_s, gates_all[:].rearrange("p a b -> p (a b)"),
                [("gates", i) for i in range(32)], ["gates_s"], "o_g")

    def phase_B():
        Vall = A.alloc([128, 32, 1024], BF16)
        dma(Vall, v_s.rearrange("(a p) c -> p a c", p=128), ["v_s"], ["Vall"], "b_v")
        qT = [A.alloc([96, S], BF16) for _ in range(2)]
        kT = [A.alloc([96, S], BF16) for _ in range(2)]
        NPT = 4
        SB = [0, 2, 6]
        pT = [A.alloc([128, 1024], BF16) for _ in range(NPT)]
        o_sb = [A.alloc([96, 512], F32) for _ in range(2)]
        rs = A.alloc([96, 512], F32)
        yT = [A.alloc([64, S], BF16) for _ in range(2)]
        scale = float(96 ** -0.5)

        def load_head(h):
            dma(qT[h % 2], qkT_s[h], ["qkT_s"], [("qT", h % 2)], ("qT", h % 2))
            dma(kT[h % 2], qkT_s[8 + h], ["qkT_s"], [("kT", h % 2)], ("kT", h % 2), q="pool")

        load_head(0)
        pending = []
        blk = 0
        cnt = [0]
        for h in range(8):
            if h + 1 < 8:
                load_head(h + 1)
            q_, k_, y_ = qT[h % 2], kT[h % 2], yT[h % 2]
            qk_, kk_, yk_ = ("qT", h % 2), ("kT", h % 2), ("yT", h % 2)
            for j in range(8):
                po, pko = ps[4 + blk % 2], ("ps", 4 + blk % 2)
                osb, osk = o_sb[blk % 2], ("o_sb", blk % 2)
                blk += 1
                qs = slice(j * 512, (j + 1) * 512)

                def s_pair(ii, q_=q_, k_=k_, qs=qs, qk_=qk_, kk_=kk_):
                    n = cnt[0]
                    cnt[0] += 1
                    b = SB[n % 3]

                    def fn(e, b=b, ii=ii):
                        e.matmul(ps[b], k_[:, (2 * ii) * 128:(2 * ii + 1) * 128], q_[:, qs], start=True, stop=True)
                        return e.matmul(ps[b + 1], k_[:, (2 * ii + 1) * 128:(2 * ii + 2) * 128], q_[:, qs], start=True, stop=True)
                    P.add("pe", fn, r=[qk_, kk_], w=[("psp", b)])
                    return n
                ns = [s_pair(0), s_pair(1)]
                for ii in range(16):
                    if ii + 2 < 16:
                        ns.append(s_pair(ii + 2))
                    n = ns[ii]
                    b = SB[n % 3]
                    pt_ = pT[n % NPT]
                    ptk = ("pT", n % NPT)
                    P.add("act", lambda e, b=b, pt_=pt_: e.activation(pt_, psbig[:, b * 512:(b + 2) * 512], AF.Exp, scale=scale),
                          r=[("psp", b)], w=[ptk])

                    def pv(e, pt_=pt_, ii=ii, po=po, h=h):
                        e.matmul(po[:, :], Vall[:, 2 * ii, h * 128:(h + 1) * 128], pt_[:, 0:512], start=(ii == 0), stop=False)
                        return e.matmul(po[:, :], Vall[:, 2 * ii + 1, h * 128:(h + 1) * 128], pt_[:, 512:1024],
                                        start=False, stop=(ii == 15))
                    P.add("pe", pv, r=[ptk, "Vall"], w=[pko])
                    if ii == 7 and pending:
                        pending.pop(0)()

                def fin(po=po, pko=pko, osb=osb, osk=osk, y_=y_, yk_=yk_, qs=qs, j=j, h=h):
                    P.add("dve", lambda e: e.tensor_copy(osb[0:96, :], po[0:96, :]), r=[pko], w=[osk])
                    P.add("dve", lambda e: e.reciprocal(rs[64:65, :], osb[64:65, :]), r=[osk], w=["rs"])
                    P.add("pe", lambda e: e.matmul(po[0:64, :], onesf[64:65, 0:64], rs[64:65, :], start=True, stop=True),
                          r=["rs", "onesf"], w=[pko])
                    P.add("dve", lambda e: e.tensor_tensor(y_[0:64, qs], osb[0:64, :], po[0:64, :], ALU.mult),
                          r=[osk, pko], w=[(yk_, j), pko])
                    if j == 7:
                        dma(yT_s[h * 64:(h + 1) * 64, :], y_[0:64, :], [(yk_, jj) for jj in range(8)], ["yT_s"], "o_yT")
                pending.append(fin)
        while pending:
            pending.pop(0)()

    def phase_C():
        vml = A.alloc([128, 32, 516], BF16)
        dma(vml, vml_s.rearrange("(a p) c -> p a c", p=128), ["vml_s"], ["vml"], "c_vml")
        gmn = A.alloc([128, 512], F32)
        dma(gmn, mnorm_d.partition_broadcast(128), [], ["gmn"], "c_gmn")
        L = A.alloc([128, 2, 32, 4], F32)
        E = A.alloc([128, 2, 32, 4], F32)
        Tsb = A.alloc([128, 2, 32, 4], F32)
        Dm = A.alloc([128, 2, 32, 4], F32)
        Wp = A.alloc([128, 2, 32, 4], F32)
        Fi = A.alloc([128, 2, 32, 4], F32)
        Ec = A.alloc([128, 2, 32, 4], F32)
        gk = [("gates", i) for i in range(32)]
        for d in range(2):
            fsl = gates_all[:, :, 4 + 8 * d:8 + 8 * d]
            isl = gates_all[:, :, 8 * d:4 + 8 * d]
            P.add("act", lambda e, d=d, fsl=fsl: e.activation(E[:, d], fsl, AF.Exp, scale=-1.0), r=gk, w=[("E", d)])
            P.add("act", lambda e, d=d: e.activation(L[:, d], E[:, d], AF.Ln, bias=1.0), r=[("E", d)], w=[("L", d)])
            Lf = L[:, d].rearrange("p a b -> p (a b)")
            pb_, pkb_ = psum()
            pt_, pkt_ = psum()
            um = uincl if d == 0 else uinclT
            P.add("pe", lambda e, pb_=pb_, um=um, Lf=Lf: e.matmul(pb_[:, 0:128], um, Lf, start=True, stop=True),
                  r=[("L", d), "uincl", "uinclT"], w=[pkb_])
            P.add("pe", lambda e, pt_=pt_, Lf=Lf: e.matmul(pt_[:, 0:128], onesf, Lf, start=True, stop=True),
                  r=[("L", d), "onesf"], w=[pkt_])
            Tf = Tsb[:, d].rearrange("p a b -> p (a b)")
            Df = Dm[:, d].rearrange("p a b -> p (a b)")
            P.add("act", lambda e, pt_=pt_, Tf=Tf: e.copy(Tf, pt_[:, 0:128]), r=[pkt_], w=[("T", d)])
            P.add("dve", lambda e, pb_=pb_, Tf=Tf, Df=Df: e.tensor_tensor(Df, pb_[:, 0:128], Tf, ALU.subtract),
                  r=[pkb_, ("T", d)], w=[("Dm", d)])
            P.add("act", lambda e, d=d: e.activation(Fi[:, d], Dm[:, d], AF.Exp), r=[("Dm", d)], w=[("Ft", d)])
            P.add("act", lambda e, d=d: e.activation(Ec[:, d], Tsb[:, d], AF.Exp, scale=-1.0), r=[("T", d)], w=[("Ec", d)])
            P.add("dve", lambda e, d=d, isl=isl: e.tensor_tensor(E[:, d], Dm[:, d], isl, ALU.add),
                  r=[("Dm", d), ("L", d)] + gk, w=[("E", d)])
            P.add("act", lambda e, d=d: e.activation(Wp[:, d], E[:, d], AF.Exp), r=[("E", d)], w=[("Wp", d)])

        cK = float(128 ** -0.5)
        ucf = A.alloc([128, 128], F32)
        ucb = A.alloc([128, 128], F32)
        P.add("dve", lambda e: e.tensor_scalar(ucf, uincl, cK, None, ALU.mult), r=["uincl"], w=["ucf"])
        P.add("dve", lambda e: e.tensor_scalar(ucb, uinclT, cK, None, ALU.mult), r=["uinclT"], w=["ucb"])
        Ecc = A.alloc([128, 2, 32, 4], F32)
        P.add("dve", lambda e: e.tensor_scalar(Ecc[:].rearrange("p a b c -> p (a b c)"), Ec[:].rearrange("p a b c -> p (a b c)"),
                                               cK, None, ALU.mult), r=[("Ec", 0), ("Ec", 1)], w=["Ecc"])
        qTm = A.alloc([128, S], BF16)
        kTm = A.alloc([128, S], BF16)
        ktok = A.alloc([128, 32, 128], BF16)
        o_h = A.alloc([128, 32, 128], F32)
        vt_all = [A.alloc([128, 32, 129], BF16) for _ in range(2)]
        hraw2 = [[A.alloc([128, 32, 129], F32) for _ in range(2)] for _ in range(2)]
        yb = A.alloc([128, 32, 128], BF16)
        yTh = A.alloc([128, S], BF16)
        CT32 = [A.alloc([128, 129], F32) for _ in range(2)]
        CTs = [[A.alloc([128, 129], BF16) for _ in range(2)] for _ in range(2)]
        sTm = [A.alloc([128, 128], BF16) for _ in range(4)]
        nd_ = A.alloc([128, 32, 1], F32)
        rr_ = [A.alloc([128, 32, 1], F32) for _ in range(2)]
        ssq = A.alloc([128, 32, 1], F32)
        ssd = A.alloc([128, 32, 1], F32)
        srs = A.alloc([128, 32, 1], F32)
        gate_keys = [("Wp", 0), ("Wp", 1), ("Ft", 0), ("Ft", 1), ("Ec", 0), ("Ec", 1), "Ecc"]
        step = 0
        def prep(h):
            dma(qTm, qkmT_s[h], ["qkmT_s"], ["qTm"], "c_zq")
            dma(kTm, qkmT_s[4 + h], ["qkmT_s"], ["kTm"], "c_zk", q="pool")
            for c4 in range(8):
                pt, pk = psum()
                ptv = pt[:].bitcast(BF16).rearrange("p (a b) -> p a b", a=8)

                def trk(e, ptv=ptv, c4=c4):
                    for a in range(4):
                        c = c4 * 4 + a
                        ins = e.transpose(ptv[:, a, :], kTm[:, c * 128:(c + 1) * 128], identb)
                    return ins
                P.add("pe", trk, r=["kTm", "identb"], w=[pk])
                P.add("act", lambda e, ptv=ptv, c4=c4: e.copy(ktok[:, c4 * 4:(c4 + 1) * 4, :], ptv[:, 0:4, :]),
                      r=[pk], w=[("ktok", c4)])
            kkeys = [("ktok", c4) for c4 in range(8)]
            vview = vml[:, :, h * 129:(h + 1) * 129]
            for d in range(2):
                wb_ = Wp[:, d, :, h:h + 1].to_broadcast([128, 32, 129])
                P.add("pool", lambda e, d=d, wb_=wb_, vview=vview: e.tensor_tensor(vt_all[d], vview, wb_, ALU.mult),
                      r=["vml"] + gate_keys, w=[("vt_all", d)])
                P.add("pool", lambda e, d=d: e.memset(CT32[d], 0.0), w=[("CT32", d)])
                P.add("pool", lambda e, d=d: e.memset(CTs[d][0], 0.0), w=[("CTs", d, 0)])

        def loop(h, pend):
            hraw = hraw2[h % 2]
            hs_ = h % 2
            kkeys = [("ktok", c4) for c4 in range(8)]
            orders = [list(range(32)), list(range(31, -1, -1))]
            for n in range(32):
                for d in range(2):
                    um = ucf if d == 0 else ucb
                    order = orders[d]
                    c = order[n]
                    csl = slice(c * 128, (c + 1) * 128)
                    sl_ = step_box[0] % 4
                    step_box[0] += 1
                    pS, pkS = psum()
                    P.add("pe", lambda e, pS=pS, csl=csl: e.matmul(pS[:, 0:128], kTm[:, csl], qTm[:, csl], start=True, stop=True),
                          r=["kTm", "qTm"], w=[pkS])
                    P.add("dve", lambda e, pS=pS, sl_=sl_, um=um: e.tensor_tensor(sTm[sl_], pS[:, 0:128], um, ALU.mult),
                          r=[pkS, "ucf", "ucb"], w=[("sTm", sl_)])
                    pK, pkK = psum()
                    P.add("pe", lambda e, pK=pK, c=c, d=d: e.matmul(pK[:, 0:129], ktok[:, c, :], vt_all[d][:, c, :], start=True, stop=True),
                          r=kkeys + [("vt_all", d)], w=[pkK])
                    pO, pkO = psum()

                    def mmo(e, pO=pO, sl_=sl_, csl=csl, n=n, d=d, c=c):
                        e.matmul(pO[:, 0:129], sTm[sl_], vt_all[d][:, c, :], start=True, stop=False)
                        return e.matmul(pO[:, 0:129], qTm[:, csl], CTs[d][n % 2], start=False, stop=True)
                    P.add("pe", mmo, r=[("sTm", sl_), ("vt_all", d), "qTm", ("CTs", d, n % 2)], w=[pkO])
                    P.add("dve", lambda e, pK=pK, c=c, h=h, d=d: e.scalar_tensor_tensor(
                        CT32[d], CT32[d], Ec[:, d, c, h:h + 1], pK[:, 0:129], ALU.mult, ALU.add),
                        r=[("CT32", d), pkK] + gate_keys, w=[("CT32", d)])
                    if n + 1 < 32:
                        cn = order[n + 1]
                        P.add("act", lambda e, n=n, cn=cn, h=h, d=d: e.activation(
                            CTs[d][(n + 1) % 2], CT32[d], AF.Copy, scale=Ecc[:, d, cn, h:h + 1]),
                            r=[("CT32", d)] + gate_keys, w=[("CTs", d, (n + 1) % 2)])
                    P.add("act", lambda e, pO=pO, d=d, c=c, hraw=hraw: e.copy(hraw[d][:, c, :], pO[:, 0:129]), r=[pkO],
                          w=[("hraw", hs_, d, c)])
                    if pend:
                        pend.pop(0)()
            while pend:
                pend.pop(0)()

        def load_oh(h):
            dma(o_h, o_s[:, h * 128:(h + 1) * 128].rearrange("(a p) c -> p a c", p=128), ["o_s"], ["o_h"], "c_oh", q="pool")
            gm = gmn[:, h * 128:(h + 1) * 128].unsqueeze(1).to_broadcast([128, 32, 128])
            P.add("pool", lambda e, gm=gm: e.tensor_tensor(o_h, o_h, gm, ALU.mult), r=["o_h", "gmn"], w=["o_h"])

        def post_ops(h):
            hs_ = h % 2
            hr = hraw2[hs_]
            H0 = hr[0][:, :, 0:128]
            H1 = hr[1][:, :, 0:128]
            ops = []
            for d in (1, 0):
                hk_d = [("hraw", hs_, d, c) for c in range(32)]
                den = hr[d][:, :, 128:129]
                fi = Fi[:, d, :, h:h + 1]
                ops.append(lambda den=den, hk_d=hk_d: P.add("dve", lambda e: e.tensor_scalar(nd_, den, -1.0, None, ALU.mult), r=hk_d, w=["nd_"]))
                ops.append(lambda den=den, hk_d=hk_d: P.add("dve", lambda e: e.tensor_tensor(nd_, nd_, den, ALU.max), r=hk_d + ["nd_"], w=["nd_"]))
                ops.append(lambda fi=fi: P.add("dve", lambda e: e.tensor_tensor(nd_, nd_, fi, ALU.max), r=["nd_"] + gate_keys, w=["nd_"]))
                ops.append(lambda d=d: P.add("dve", lambda e: e.reciprocal(rr_[d], nd_), r=["nd_"], w=[("rr", d)]))
            hk0 = [("hraw", hs_, 0, c) for c in range(32)]
            hk1 = [("hraw", hs_, 1, c) for c in range(32)]
            K0, K1 = ("H", hs_, 0), ("H", hs_, 1)
            ops.append(lambda: P.add("pool", lambda e: e.tensor_tensor(H1, H1, rr_[1].to_broadcast([128, 32, 128]), ALU.mult),
                                     r=hk1 + [("rr", 1)], w=hk1 + [K1]))
            ops.append(lambda: P.add("dve", lambda e: e.tensor_tensor(H0, H0, rr_[0].to_broadcast([128, 32, 128]), ALU.mult),
                                     r=hk0 + [("rr", 0)], w=hk0 + [K0]))
            ops.append(lambda: P.add("dve", lambda e: e.tensor_tensor(H0, H0, H1, ALU.add), r=[K0, K1], w=[K0]))
            ops.append(lambda: P.add("act", lambda e: e.activation(H1, H0, AF.Square), r=[K0, K1], w=[K1]))
            ops.append(lambda: P.add("dve", lambda e: e.tensor_reduce(ssq[:, :, 0], H1, AX.X, ALU.add), r=[K1], w=["ssq"]))
            ops.append(lambda: P.add("act", lambda e: e.activation(ssd, ssq, AF.Sqrt, bias=EPS, scale=1.0 / 128), r=["ssq"], w=["ssd"]))
            ops.append(lambda: P.add("dve", lambda e: e.reciprocal(srs, ssd), r=["ssd"], w=["srs"]))
            ops.append(lambda: P.add("dve", lambda e: e.tensor_tensor(H0, H0, srs.to_broadcast([128, 32, 128]), ALU.mult),
                                     r=[K0, "srs"], w=[K0]))
            ops.append(lambda: P.add("dve", lambda e: e.tensor_tensor(yb, H0, o_h, ALU.mult), r=[K0, "o_h"], w=["yb"] + hk0 + hk1))
            if h + 1 < 4:
                ops.append(lambda: load_oh(h + 1))
            for c4 in range(8):
                def trg(c4=c4):
                    pt, pk = psum()
                    ptv = pt[:].bitcast(BF16).rearrange("p (a b) -> p a b", a=8)

                    def try_(e, ptv=ptv, c4=c4):
                        for a in range(4):
                            ins = e.transpose(ptv[:, a, :], yb[:, c4 * 4 + a, :], identb)
                        return ins
                    P.add("pe", try_, r=["yb", "identb"], w=[pk])
                    P.add("act", lambda e, ptv=ptv, c4=c4: e.copy(
                        yTh[:, c4 * 512:(c4 + 1) * 512].rearrange("p (a b) -> p a b", a=4), ptv[:, 0:4, :]),
                        r=[pk], w=[("yTh", c4)])
                ops.append(trg)
            ops.append(lambda: dma(yT_s[512 + h * 128:512 + (h + 1) * 128, :], yTh, [("yTh", c4) for c4 in range(8)], ["yT_s"], "o_yT2"))
            return ops

        step_box = [0]
        prep(0)
        load_oh(0)
        pend = []
        for h in range(4):
            loop(h, pend)
            if h + 1 < 4:
                prep(h + 1)
            pend = post_ops(h)
        while pend:
            pend.pop(0)()

    wl_n = [0]
    PW = 640

    WPW = {}

    def load_w(dst, src_d, KC, N, gain, wkey, stage=None, skey=None):
        assert gain is None
        npc = (N + 2047) // 2048
        pw = (N + npc - 1) // npc
        WPW[wkey] = pw
        for pi in range(npc):
            c0, c1 = pi * pw, min(N, (pi + 1) * pw)
            for kc in range(KC):
                dma(dst[:, kc, c0:c1], src_d[kc * 128:(kc + 1) * 128, c0:c1], [], [(wkey, pi, kc)], (wkey + "_d", pi), q="pool")

    def wkeys(wkey, KC, N, c0=0, c1=None):
        c1 = N if c1 is None else c1
        pw = WPW[wkey]
        pis = range(c0 // pw, (c1 - 1) // pw + 1)
        return [(wkey, pi, kc) for pi in pis for kc in range(KC)]

    def gain_tile(src_d, n, name):
        t = A.alloc([128, n], F32)
        dma(t, src_d.rearrange("(o d) -> o d", o=1).partition_broadcast(128), [], [name], "gt_" + name)
        return t

    def norm_transpose(xs, xk, xnb, xnk, junk, stats, dstT, dkey, pfx, col=0, gain=None, defer=False, junk_key="junk"):
        xks = list(xk) if isinstance(xk, list) else [xk]
        pfx = pfx + str(col)
        c0 = 3 * col
        P.add("act", lambda e: e.activation(junk, xs, AF.Square, accum_out=stats[:, c0:c0 + 1]), r=xks, w=[junk_key, pfx + "ss"])
        rstd_from_ss(stats[:, c0:c0 + 1], stats[:, c0 + 1:c0 + 2], stats[:, c0 + 2:c0 + 3], D, pfx)
        if gain is None:
            P.add("dve", lambda e: e.tensor_scalar(xnb, xs, stats[:, c0 + 2:c0 + 3], None, ALU.mult), r=xks + [pfx + "rstd"], w=[xnk])
        else:
            gt_, gk_ = gain
            P.add("dve", lambda e: e.scalar_tensor_tensor(xnb, xs, stats[:, c0 + 2:c0 + 3], gt_, ALU.mult, ALU.mult),
                  r=xks + [pfx + "rstd", gk_], w=[xnk])
        def tpart():
            pt, pk = psum()
            ptv = pt[:].bitcast(BF16).rearrange("p (a b) -> p a b", a=8)

            def tr(e):
                for kc in range(8):
                    ins = e.transpose(ptv[:, kc, :], xnb[:, kc * 128:(kc + 1) * 128], identb)
                return ins
            P.add("pe", tr, r=[xnk, "identb"], w=[pk])
            P.add("act", lambda e: e.copy(dstT, ptv), r=[pk], w=[dkey])
        if defer:
            return tpart
        tpart()

    def phase_D():
        Wo = A.alloc([128, 8, 1024], BF16)
        Wxq = A.alloc([128, 8, 1024], BF16)
        Wxo = A.alloc([128, 8, 1024], BF16)
        kxT = A.alloc([128, 8, 256], BF16)
        vx = A.alloc([128, 2, 1024], BF16)
        onesb = A.alloc([128, 128], BF16)
        P.add("pool", lambda e: e.memset(onesb, 1.0), w=["onesb"])
        gX = gain_tile(xattn_norm_d, D, "gX")
        gM = gain_tile(mem_norm_d, D, "gM")
        stats = A.alloc([128, 16], F32)
        junk = A.alloc([128, D], BF16)
        xn = [A.alloc([128, D], BF16) for _ in range(4)]
        xt = [A.alloc([128, D], F32) for _ in range(2)]
        Wxkv = A.alloc([128, 8, 2048], BF16)
        memnT = A.alloc([128, 8, 256], BF16)
        xm = [A.alloc([128, D], F32) for _ in range(2)]
        load_w(Wo, w_out_d, 8, 1024, None, "Wo")
        load_w(Wxkv, w_xkv_d, 8, 2048, None, "Wxkv")
        load_w(Wxq, w_xq_d, 8, 1024, None, "Wxq")
        load_w(Wxo, w_xo_d, 8, 1024, None, "Wxo")
        for mb in range(2):
            xs, xk = xm[mb], ("xm", mb)
            dma(xs, mem_d[mb * 128:(mb + 1) * 128, :], [], [xk], xk)
            norm_transpose(xs, xk, xn[mb], ("xn", mb), junk, stats, memnT[:, :, mb * 128:(mb + 1) * 128], ("memnT", mb), "m", gain=(gM, "gM"))
        mk = [("memnT", 0), ("memnT", 1)]
        wk_ = wkeys("Wxkv", 8, 2048)
        for hj in range(8):
            pt, pk = psum()

            def mmk(e, pt=pt, hj=hj):
                for kc in range(8):
                    ins = e.matmul(pt[:, 0:256], Wxkv[:, kc, hj * 128:(hj + 1) * 128], memnT[:, kc, :], start=(kc == 0), stop=(kc == 7))
                return ins
            P.add("pe", mmk, r=mk + wk_, w=[pk])
            P.add("act", lambda e, pt=pt, hj=hj: e.copy(kxT[:, hj, :], pt[:, 0:256]), r=[pk], w=[("kxT", hj)])
        for mb in range(2):
            for half in range(2):
                pt, pk = psum()

                def mmv(e, pt=pt, mb=mb, half=half):
                    for kc in range(8):
                        ins = e.matmul(pt[:, :], memnT[:, kc, mb * 128:(mb + 1) * 128],
                                       Wxkv[:, kc, 1024 + half * 512:1024 + (half + 1) * 512], start=(kc == 0), stop=(kc == 7))
                    return ins
                P.add("pe", mmv, r=mk + wk_, w=[pk])
                P.add("dve", lambda e, pt=pt, mb=mb, half=half: e.tensor_copy(vx[:, mb, half * 512:(half + 1) * 512], pt[:, :]),
                      r=[pk], w=[("vx", mb, half)])
        yT_st = A.alloc([128, 8, 512], BF16)
        x1 = A.alloc([128, 4, D], F32)
        h1T = A.alloc([128, 8, 512], BF16)
        qxT = A.alloc([128, 8, 512], BF16)
        oxT = A.alloc([128, 8, 512], BF16)
        pX = [A.alloc([128, 512], BF16) for _ in range(4)]
        rsum = [A.alloc([128, 512], F32) for _ in range(2)]
        xscale = float(256 ** -0.5)
        xi = 0
        for st_i in range(NST):
            cs = slice(st_i * 512, (st_i + 1) * 512)
            dma(yT_st, yT_s[:, cs].rearrange("(c p) t -> p c t", p=128), ["yT_s"], ["yT_st"], "d_yT")
            for t in range(4):
                tile = st_i * 4 + t
                ts_ = slice(t * 128, (t + 1) * 128)
                xs, xk = xt[xi % 2], ("xt", xi % 2)
                xnb, xnk = xn[xi % 2], ("xn", xi % 2)
                xi += 1
                dma(xs, x_d[tile * 128:(tile + 1) * 128, :], [], [xk], xk)
                for half in range(2):
                    pt, pk = psum()
                    hs = slice(half * 512, (half + 1) * 512)

                    def mmo(e, pt=pt, ts_=ts_, hs=hs):
                        for kc in range(8):
                            ins = e.matmul(pt[:, :], yT_st[:, kc, ts_], Wo[:, kc, hs], start=(kc == 0), stop=(kc == 7))
                        return ins
                    P.add("pe", mmo, r=["yT_st"] + wkeys("Wo", 8, 1024), w=[pk])
                    P.add("dve", lambda e, pt=pt, t=t, hs=hs, xs=xs: e.tensor_tensor(x1[:, t, hs], pt[:, :], xs[:, hs], ALU.add),
                          r=[pk, xk], w=[("x1", t, half)])
            tps = []
            for t in range(4):
                ts_ = slice(t * 128, (t + 1) * 128)
                tps.append(norm_transpose(x1[:, t, :], [("x1", t, 0), ("x1", t, 1)], xn[t], ("xn", t), junk, stats,
                                          h1T[:, :, ts_], ("h1T", t), "d", col=t, gain=(gX, "gX"), defer=True))
            for tp_ in tps:
                tp_()
            hk = [("h1T", t) for t in range(4)]
            for hj in range(8):
                pt, pk = psum()

                def mmq(e, pt=pt, hj=hj):
                    for kc in range(8):
                        ins = e.matmul(pt[:, :], Wxq[:, kc, hj * 128:(hj + 1) * 128], h1T[:, kc, :], start=(kc == 0), stop=(kc == 7))
                    return ins
                P.add("pe", mmq, r=hk + wkeys("Wxq", 8, 1024), w=[pk])
                evac_d(qxT[:, hj, :], pt[:, :], [pk], [("qxT", hj)])
            def emit_S(h):
                for mb in range(2):
                    pt, pk = psum()

                    def mms(e, pt=pt, h=h, mb=mb):
                        for j in range(2):
                            ins = e.matmul(pt[:, :], kxT[:, h * 2 + j, mb * 128:(mb + 1) * 128], qxT[:, h * 2 + j, :],
                                           start=(j == 0), stop=(j == 1))
                        return ins
                    P.add("pe", mms, r=[("qxT", h * 2), ("qxT", h * 2 + 1)] + [("kxT", hj_) for hj_ in range(8)], w=[pk])
                    px_ = pX[(h % 2) * 2 + mb]
                    P.add("act", lambda e, pt=pt, px_=px_: e.activation(px_, pt[:, :], AF.Exp, scale=xscale),
                          r=[pk], w=[("pX", h % 2, mb)])

            def emit_rest(h):
                p0, p1 = pX[(h % 2) * 2], pX[(h % 2) * 2 + 1]
                pxk = [("pX", h % 2, 0), ("pX", h % 2, 1)]
                rs_ = rsum[h % 2]
                rk_ = ("rsum", h % 2)
                pt, pk = psum()

                def mmsum(e, pt=pt):
                    e.matmul(pt[:, :], onesb, p0, start=True, stop=False)
                    return e.matmul(pt[:, :], onesb, p1, start=False, stop=True)
                P.add("pe", mmsum, r=pxk + ["onesb"], w=[pk])
                P.add("dve", lambda e, pt=pt: e.reciprocal(rs_, pt[:, :]), r=[pk], w=[rk_])
                for j in range(2):
                    pt2, pk2 = psum()
                    hj = h * 2 + j

                    def mmov(e, pt2=pt2, hj=hj):
                        e.matmul(pt2[:, :], vx[:, 0, hj * 128:(hj + 1) * 128], p0, start=True, stop=False)
                        return e.matmul(pt2[:, :], vx[:, 1, hj * 128:(hj + 1) * 128], p1, start=False, stop=True)
                    P.add("pe", mmov, r=pxk + [("vx", mb_, hf_) for mb_ in range(2) for hf_ in range(2)], w=[pk2])
                    P.add("dve", lambda e, pt2=pt2, hj=hj: e.tensor_tensor(oxT[:, hj, :], pt2[:, :], rs_, ALU.mult),
                          r=[pk2, rk_], w=[("oxT", hj)])

            emit_S(0)
            for h in range(4):
                if h + 1 < 4:
                    emit_S(h + 1)
                emit_rest(h)
            ok = [("oxT", hj) for hj in range(8)]
            for t in range(4):
                tile = st_i * 4 + t
                ts_ = slice(t * 128, (t + 1) * 128)
                for half in range(2):
                    pt, pk = psum()
                    hs = slice(half * 512, (half + 1) * 512)

                    def mmxo(e, pt=pt, ts_=ts_, hs=hs):
                        for hj in range(8):
                            ins = e.matmul(pt[:, :], oxT[:, hj, ts_], Wxo[:, hj, hs], start=(hj == 0), stop=(hj == 7))
                        return ins
                    P.add("pe", mmxo, r=ok + wkeys("Wxo", 8, 1024), w=[pk])
                    P.add("dve", lambda e, pt=pt, t=t, hs=hs: e.tensor_tensor(x1[:, t, hs], pt[:, :], x1[:, t, hs], ALU.add),
                          r=[pk, ("x1", t, half)], w=[("x1", t, half)])
                dma(x2_s[tile * 128:(tile + 1) * 128, :], x1[:, t, :], [("x1", t, 0), ("x1", t, 1)], ["x2_s"], "o_x2", q="pool")

    evd_rr = [0]

    def evac_d(out, in_, r, w):
        eng = "act"
        evd_rr[0] += 1
        if eng == "act":
            P.add("act", lambda e: e.copy(out, in_), r=r, w=w)
        else:
            P.add("dve", lambda e: e.tensor_copy(out, in_), r=r, w=w)

    def phase_E():
        Wgu = A.alloc([128, 8, 2 * DFF], BF16)
        Wd = A.alloc([128, 22, 1024], BF16)
        fgain = A.alloc([128, D], F32)
        dma(fgain, fnorm_d.partition_broadcast(128), [], ["fgain"], "e_fg")
        stats = A.alloc([128, 16], F32)
        gtile = A.alloc([128, D], F32)
        dma(gtile, ffn_norm_d.rearrange("(o d) -> o d", o=1).partition_broadcast(128), [], ["gtile"], "e_gt")
        GP = 1408
        for pi in (0, 2, 1, 3):
            for kc in range(8):
                dma(Wgu[:, kc, pi * GP:(pi + 1) * GP], w_gu_d[kc * 128:(kc + 1) * 128, pi * GP:(pi + 1) * GP],
                    [], [("Wgu", pi, kc)], ("wgu", pi), q="pool")
        for kc in range(22):
            dma(Wd[:, kc, :], w_down_d[kc * 128:(kc + 1) * 128, :], [], [("Wd", 0, kc)], "wd", q="pool")

        def gkeys(c0, c1):
            return [("Wgu", pi, kc) for pi in range(c0 // GP, (c1 - 1) // GP + 1) for kc in range(8)]
        xt = [A.alloc([128, D], F32) for _ in range(2)]
        xn = [A.alloc([128, D], BF16) for _ in range(4)]
        h2T = A.alloc([128, 8, 512], BF16)
        actT = A.alloc([128, 22, 512], BF16)
        gsb = [A.alloc([128, 512], F32) for _ in range(2)]
        xi = 0
        gi = 0
        xl = [A.alloc([128, D], F32) for _ in range(3)]
        li = [0]

        def load_norm(st_i):
            tps = []
            for t in range(4):
                tile = st_i * 4 + t
                xs, xk = xl[li[0] % 3], ("xl", li[0] % 3)
                xnb, xnk = xn[li[0] % 4], ("xn", li[0] % 4)
                li[0] += 1
                dma(xs, x2_s[tile * 128:(tile + 1) * 128, :], ["x2_s"], [xk], xk)
                tps.append(norm_transpose(xs, xk, xnb, xnk, xnb, stats, h2T[:, :, t * 128:(t + 1) * 128], ("h2T", t), "e", col=t,
                                          gain=(gtile, "gtile"), defer=True, junk_key=xnk))
            return tps

        for tp_ in load_norm(0):
            tp_()
        for st_i in range(NST):
            hk = [("h2T", t) for t in range(4)]
            next_tps = load_norm(st_i + 1) if st_i + 1 < NST else []
            for f in range(22):
                pg, pkg = psum()
                pu, pku = psum()

                def mmg(e, pg=pg, f=f):
                    for kc in range(8):
                        ins = e.matmul(pg[:, :], Wgu[:, kc, f * 128:(f + 1) * 128], h2T[:, kc, :], start=(kc == 0), stop=(kc == 7))
                    return ins

                def mmu(e, pu=pu, f=f):
                    for kc in range(8):
                        ins = e.matmul(pu[:, :], Wgu[:, kc, DFF + f * 128:DFF + (f + 1) * 128], h2T[:, kc, :],
                                       start=(kc == 0), stop=(kc == 7))
                    return ins
                P.add("pe", mmg, r=hk + gkeys(f * 128, (f + 1) * 128), w=[pkg])
                P.add("pe", mmu, r=hk + gkeys(DFF + f * 128, DFF + (f + 1) * 128), w=[pku])
                gb, gk_ = gsb[gi % 2], ("gsb", gi % 2)
                gi += 1
                P.add("act", lambda e, pg=pg, gb=gb: e.activation(gb, pg[:, :], AF.Silu), r=[pkg], w=[gk_])
                P.add("dve", lambda e, pu=pu, gb=gb, f=f: e.tensor_tensor(actT[:, f, :], pu[:, :], gb, ALU.mult),
                      r=[pku, gk_], w=[("actT", f)])
            ak = [("actT", f) for f in range(22)]
            for tp_ in next_tps:
                tp_()
            for t in range(4):
                tile = st_i * 4 + t
                ts_ = slice(t * 128, (t + 1) * 128)
                xs, xk = xt[xi % 2], ("xt", xi % 2)
                xi += 1
                dma(xs, x2_s[tile * 128:(tile + 1) * 128, :], ["x2_s"], [xk], xk)
                for half in range(2):
                    pt, pk = psum()
                    hs = slice(half * 512, (half + 1) * 512)

                    def mmd(e, pt=pt, ts_=ts_, hs=hs):
                        for f in range(22):
                            ins = e.matmul(pt[:, :], actT[:, f, ts_], Wd[:, f, hs], start=(f == 0), stop=(f == 21))
                        return ins
                    P.add("pe", mmd, r=ak + [("Wd", 0, kc) for kc in range(22)], w=[pk])
                    P.add("dve", lambda e, pt=pt, xs=xs, hs=hs: e.tensor_tensor(xs[:, hs], pt[:, :], xs[:, hs], ALU.add),
                          r=[pk, xk], w=[xk])
                P.add("act", lambda e, xs=xs: e.activation(gsb[0][:].bitcast(BF16), xs, AF.Square, accum_out=stats[:, 12:13]),
                      r=[xk], w=[("gsb", 0), "fss"])
                P.add("act", lambda e: e.activation(stats[:, 13:14], stats[:, 12:13], AF.Sqrt, bias=EPS, scale=1.0 / D),
                      r=["fss"], w=["fsd"])
                P.add("dve", lambda e: e.reciprocal(stats[:, 14:15], stats[:, 13:14]), r=["fsd"], w=["frs"])
                P.add("dve", lambda e, xs=xs: e.scalar_tensor_tensor(xs, xs, stats[:, 14:15], fgain, ALU.mult, ALU.mult),
                      r=[xk, "frs", "fgain"], w=[xk])
                dma(out_d[tile * 128:(tile + 1) * 128, :], xs, [xk], ["out"], "o_out", q="pool")

    phase_A()
    P.barrier()
    A.reset()
    if STOP >= 2:
        phase_B()
        P.barrier()
        A.reset()
    if STOP >= 3:
        phase_C()
        P.barrier()
        A.reset()
    if STOP >= 4:
        phase_D()
        P.barrier()
        A.reset()
    if STOP >= 5:
        phase_E()

    P.emit(nc, stack)
    return nc, P, stack


def _consts():
    idx = np.arange(128)
    uincl = (idx[:, None] <= idx[None, :]).astype(np.float32)
    inv = (10000.0 ** (-np.arange(0, 32, 2, dtype=np.float32) / 32)).astype(np.float32)
    invf = np.zeros((128, 1), np.float32)
    invf[:, 0] = inv[idx % 16]
    return {
        "c_identb": np.eye(128, dtype=np.float32).astype(ml_dtypes.bfloat16),
        "c_identf": np.eye(128, dtype=np.float32),
        "c_uincl": uincl,
        "c_uinclT": np.ascontiguousarray(uincl.T),
        "c_invf": invf,
    }


def make_in_maps(inputs):
    c = _consts()
    f = lambda a: np.ascontiguousarray(np.asarray(a))
    shared = {
        "attn_norm": f(inputs["attn_norm"][0]), "w_in": f(inputs["w_in"][0]),
        "q_norm": f(inputs["q_norm"][0]), "w_uq": f(inputs["w_uq"][0]),
        "kv_norm": f(inputs["kv_norm"][0]), "w_ukv": f(inputs["w_ukv"][0]),
        "mlstm_conv": f(inputs["mlstm_conv"][0]), "mlstm_gate_bias": f(inputs["mlstm_gate_bias"]),
        "mlstm_norm": f(inputs["mlstm_norm"]), "w_out": f(inputs["w_out"][0]),
        "xattn_norm": f(inputs["xattn_norm"][0]), "mem_norm": f(inputs["mem_norm"][0]),
        "w_xq": f(inputs["w_xq"][0]), "w_xkv": f(inputs["w_xkv"][0]), "w_xo": f(inputs["w_xo"][0]),
        "ffn_norm": f(inputs["ffn_norm"][0]), "w_gate_up": f(inputs["w_gate_up"][0]),
        "w_down": f(inputs["w_down"][0]), "final_norm": f(np.asarray(inputs["final_norm"]).reshape(1, D)),
    }
    shared.update(c)
    maps = []
    for b in range(8):
        m = dict(shared)
        m["x"] = f(inputs["x"][b])
        m["mem"] = f(inputs["mem"][b])
        m["positions"] = f(np.asarray(inputs["positions"][b]).reshape(1, S).astype(np.int32))
        maps.append(m)
    return maps


def kernel(**inputs):
    nc, P, stack = build_nc()
    with stack:
        res = run_bass_kernel_spmd(nc, make_in_maps(inputs), core_ids=list(range(8)))
    out = np.stack([np.asarray(r["out"]) for r in res.results], axis=0)
    return out.astype(np.float32)
```
